# Optimizing a Trainium2 kernel written in Bass

```python
import math
import jax, jax.numpy as jnp
from jax import lax
import numpy as np

D_MODEL = 1024
BATCH = 16
SEQ = 2048
DEPTH = 2

HEAD_DIM = 64
SB_HEADS = 8
DIFF_HEADS = 4
DIFF_V_DIM = 2 * HEAD_DIM
SB_WIDTH = SB_HEADS * HEAD_DIM
DIFF_QK_WIDTH = DIFF_HEADS * 2 * HEAD_DIM
DIFF_V_WIDTH = DIFF_HEADS * DIFF_V_DIM
MIX_WIDTH = SB_WIDTH + DIFF_V_WIDTH
IN_WIDTH = 3 * SB_WIDTH + 2 * DIFF_QK_WIDTH + DIFF_V_WIDTH
CONV_WIDTH = 31
FFN_HIDDEN = -(-(8 * D_MODEL) // (3 * 256)) * 256
Q_BLOCK = 128
N_EVEN = (DEPTH + 1) // 2
N_ODD = DEPTH // 2
DEEPNORM_ALPHA = (2 * DEPTH) ** 0.25
DEEPNORM_BETA = (8 * DEPTH) ** -0.25
LN_EPS = 1e-5
ALIBI_SLOPES = np.array([2.0 ** (-8.0 * (h + 1) / DIFF_HEADS) for h in range(DIFF_HEADS)], dtype=np.float32)

kernel_name = "stickbreak_diffattn_conformer_hybrid"


def layer_norm(x, g, b):
    xf = x.astype(jnp.float32)
    mu = jnp.mean(xf, axis=-1, keepdims=True)
    var = jnp.mean(jnp.square(xf - mu), axis=-1, keepdims=True)
    return ((xf - mu) * lax.rsqrt(var + LN_EPS)).astype(x.dtype) * g + b


def rms_norm(x, g):
    xf = x.astype(jnp.float32)
    ms = jnp.mean(jnp.square(xf), axis=-1, keepdims=True)
    return (xf * lax.rsqrt(ms + LN_EPS)).astype(x.dtype) * g


def stick_breaking_block(q, k, v, t0):
    qb, sk = q.shape[1], k.shape[1]
    z = jnp.einsum('bqhd,bkhd->bhqk', q, k).astype(jnp.float32) / math.sqrt(HEAD_DIM)
    tpos = t0 + jnp.arange(qb)
    spos = jnp.arange(sk)
    strict = spos[None, :] < tpos[:, None]
    log_1mb = jnp.where(strict, jax.nn.log_sigmoid(-z), 0.0)
    between = lax.cumsum(log_1mb, axis=3, reverse=True) - log_1mb
    a = jnp.where(strict, jnp.exp(jax.nn.log_sigmoid(z) + between), 0.0)
    return jnp.einsum('bhqk,bkhd->bqhd', a.astype(v.dtype), v)


def diff_attention_block(q, k, v, t0, lam):
    qb, sk = q.shape[1], k.shape[1]
    s = jnp.einsum('bqhmd,bkhmd->bmhqk', q, k).astype(jnp.float32) / math.sqrt(HEAD_DIM)
    tpos = t0 + jnp.arange(qb)
    spos = jnp.arange(sk)
    dist = (tpos[:, None] - spos[None, :]).astype(jnp.float32)
    bias = -jnp.asarray(ALIBI_SLOPES)[:, None, None] * dist
    causal = spos[None, :] <= tpos[:, None]
    s = jnp.where(causal, s + bias, -jnp.inf)
    p = jax.nn.softmax(s, axis=-1)
    w = p[:, 0] - lam * p[:, 1]
    return jnp.einsum('bhqk,bkhe->bqhe', w.astype(v.dtype), v)


def attention_mixer(h, w_in, w_out, lq1, lk1, lq2, lk2, subln_g, lambda_init):
    bsz, seq, _ = h.shape
    proj = h @ w_in
    o0 = 0
    def take(width):
        nonlocal o0
        part = proj[..., o0:o0 + width]
        o0 += width
        return part
    q_sb = take(SB_WIDTH).reshape(bsz, seq, SB_HEADS, HEAD_DIM)
    k_sb = take(SB_WIDTH).reshape(bsz, seq, SB_HEADS, HEAD_DIM)
    v_sb = take(SB_WIDTH).reshape(bsz, seq, SB_HEADS, HEAD_DIM)
    q_df = take(DIFF_QK_WIDTH).reshape(bsz, seq, DIFF_HEADS, 2, HEAD_DIM)
    k_df = take(DIFF_QK_WIDTH).reshape(bsz, seq, DIFF_HEADS, 2, HEAD_DIM)
    v_df = take(DIFF_V_WIDTH).reshape(bsz, seq, DIFF_HEADS, DIFF_V_DIM)
    f32 = jnp.float32
    lam = (jnp.exp(jnp.sum(lq1.astype(f32) * lk1.astype(f32)))
           - jnp.exp(jnp.sum(lq2.astype(f32) * lk2.astype(f32))) + lambda_init)
    sb_out, df_out = [], []
    for blk in range(seq // Q_BLOCK):
        t0 = blk * Q_BLOCK
        t1 = t0 + Q_BLOCK
        sb_out.append(stick_breaking_block(q_sb[:, t0:t1], k_sb[:, :t1], v_sb[:, :t1], t0))
        df_out.append(diff_attention_block(q_df[:, t0:t1], k_df[:, :t1], v_df[:, :t1], t0, lam))
    o_sb = jnp.concatenate(sb_out, axis=1).reshape(bsz, seq, SB_WIDTH)
    o_df = rms_norm(jnp.concatenate(df_out, axis=1), subln_g) * (1.0 - lambda_init)
    o_df = o_df.reshape(bsz, seq, DIFF_V_WIDTH)
    return jnp.concatenate([o_sb, o_df], axis=-1) @ w_out


def conv_mixer(h, pw1_w, pw1_b, dw_w, dw_b, cln_g, cln_b, pw2_w, pw2_b):
    u = jax.nn.glu(h @ pw1_w + pw1_b, axis=-1)
    u = lax.conv_general_dilated(
        u, dw_w, window_strides=(1,), padding=[(CONV_WIDTH - 1, 0)],
        dimension_numbers=('NWC', 'WIO', 'NWC'), feature_group_count=D_MODEL) + dw_b
    u = jax.nn.silu(layer_norm(u, cln_g, cln_b))
    return u @ pw2_w + pw2_b


def swiglu(h, wg, wu, wd):
    return (jax.nn.silu(h @ wg) * (h @ wu)) @ wd


def setup_inputs(seed: int = 0) -> dict:
    key = jax.random.key(seed)
    ks = jax.random.split(key, 24)
    nrm = jax.random.normal
    f32 = jnp.float32
    x = nrm(ks[0], (BATCH, SEQ, D_MODEL), f32)
    w_in = nrm(ks[1], (N_EVEN, D_MODEL, IN_WIDTH), f32) * D_MODEL ** -0.5
    col = np.arange(IN_WIDTH)
    v_sb_cols = (col >= 2 * SB_WIDTH) & (col < 3 * SB_WIDTH)
    v_df_cols = col >= 3 * SB_WIDTH + 2 * DIFF_QK_WIDTH
    col_scale = np.where(v_sb_cols | v_df_cols, DEEPNORM_BETA, 1.0).astype(np.float32)
    w_in = w_in * jnp.asarray(col_scale)
    w_out = nrm(ks[2], (N_EVEN, MIX_WIDTH, D_MODEL), f32) * MIX_WIDTH ** -0.5 * DEEPNORM_BETA
    lq1 = nrm(ks[3], (N_EVEN, HEAD_DIM), f32) * 0.1
    lk1 = nrm(ks[4], (N_EVEN, HEAD_DIM), f32) * 0.1
    lq2 = nrm(ks[5], (N_EVEN, HEAD_DIM), f32) * 0.1
    lk2 = nrm(ks[6], (N_EVEN, HEAD_DIM), f32) * 0.1
    subln_g = 1.0 + 0.02 * nrm(ks[7], (N_EVEN, DIFF_V_DIM), f32)
    pw1_w = nrm(ks[8], (N_ODD, D_MODEL, 2 * D_MODEL), f32) * D_MODEL ** -0.5
    pw1_b = 0.02 * nrm(ks[9], (N_ODD, 2 * D_MODEL), f32)
    dw_w = nrm(ks[10], (N_ODD, CONV_WIDTH, 1, D_MODEL), f32) * CONV_WIDTH ** -0.5
    dw_b = 0.02 * nrm(ks[11], (N_ODD, D_MODEL), f32)
    cln_g = 1.0 + 0.02 * nrm(ks[12], (N_ODD, D_MODEL), f32)
    cln_b = 0.02 * nrm(ks[13], (N_ODD, D_MODEL), f32)
    pw2_w = nrm(ks[14], (N_ODD, D_MODEL, D_MODEL), f32) * D_MODEL ** -0.5 * DEEPNORM_BETA
    pw2_b = 0.02 * nrm(ks[15], (N_ODD, D_MODEL), f32)
    mix_ln_g = 1.0 + 0.02 * nrm(ks[16], (DEPTH, D_MODEL), f32)
    mix_ln_b = 0.02 * nrm(ks[17], (DEPTH, D_MODEL), f32)
    ffn_wg = nrm(ks[18], (DEPTH, D_MODEL, FFN_HIDDEN), f32) * D_MODEL ** -0.5 * DEEPNORM_BETA
    ffn_wu = nrm(ks[19], (DEPTH, D_MODEL, FFN_HIDDEN), f32) * D_MODEL ** -0.5 * DEEPNORM_BETA
    ffn_wd = nrm(ks[20], (DEPTH, FFN_HIDDEN, D_MODEL), f32) * FFN_HIDDEN ** -0.5 * DEEPNORM_BETA
    ffn_ln_g = 1.0 + 0.02 * nrm(ks[21], (DEPTH, D_MODEL), f32)
    ffn_ln_b = 0.02 * nrm(ks[22], (DEPTH, D_MODEL), f32)
    return {"x": x, "attn_w_in": w_in, "attn_w_out": w_out,
            "diff_lambda_q1": lq1, "diff_lambda_k1": lk1,
            "diff_lambda_q2": lq2, "diff_lambda_k2": lk2, "diff_subln_g": subln_g,
            "conv_pw1_w": pw1_w, "conv_pw1_b": pw1_b, "conv_dw_w": dw_w, "conv_dw_b": dw_b,
            "conv_ln_g": cln_g, "conv_ln_b": cln_b, "conv_pw2_w": pw2_w, "conv_pw2_b": pw2_b,
            "mix_ln_g": mix_ln_g, "mix_ln_b": mix_ln_b,
            "ffn_w_gate": ffn_wg, "ffn_w_up": ffn_wu, "ffn_w_down": ffn_wd,
            "ffn_ln_g": ffn_ln_g, "ffn_ln_b": ffn_ln_b}


def reference(x, attn_w_in, attn_w_out, diff_lambda_q1, diff_lambda_k1,
              diff_lambda_q2, diff_lambda_k2, diff_subln_g,
              conv_pw1_w, conv_pw1_b, conv_dw_w, conv_dw_b, conv_ln_g, conv_ln_b,
              conv_pw2_w, conv_pw2_b, mix_ln_g, mix_ln_b,
              ffn_w_gate, ffn_w_up, ffn_w_down, ffn_ln_g, ffn_ln_b):
    h = x
    for i in range(DEPTH):
        if i % 2 == 0:
            e = i // 2
            lambda_init = 0.8 - 0.6 * math.exp(-0.3 * i)
            m = attention_mixer(h, attn_w_in[e], attn_w_out[e],
                                diff_lambda_q1[e], diff_lambda_k1[e],
                                diff_lambda_q2[e], diff_lambda_k2[e],
                                diff_subln_g[e], lambda_init)
        else:
            o = i // 2
            m = conv_mixer(h, conv_pw1_w[o], conv_pw1_b[o], conv_dw_w[o], conv_dw_b[o],
                           conv_ln_g[o], conv_ln_b[o], conv_pw2_w[o], conv_pw2_b[o])
        h = layer_norm(DEEPNORM_ALPHA * h + m, mix_ln_g[i], mix_ln_b[i])
        f = swiglu(h, ffn_w_gate[i], ffn_w_up[i], ffn_w_down[i])
        h = layer_norm(DEEPNORM_ALPHA * h + f, ffn_ln_g[i], ffn_ln_b[i])
    return h
```

```python
import math
import numpy as np
import ml_dtypes
import concourse.bass as bass
import concourse.mybir as mybir
from concourse.bass_utils import run_bass_kernel_spmd

F32 = mybir.dt.float32
BF16 = mybir.dt.bfloat16
AF = mybir.ActivationFunctionType
ALU = mybir.AluOpType

D = 1024
KC = 8
FF = 2816
NFC = 22
INW = 3072
ALPHA = float(4.0 ** 0.25)
EPS = 1e-5
SLOPES = [2.0 ** (-8.0 * (h + 1) / 4) for h in range(4)]
LAMBDA_INIT = 0.8 - 0.6 * math.exp(0.0)
NEG = -30000.0
NDS_SP = 16
NDS_POOL = 8
NDS = NDS_SP + NDS_POOL
ENGS = ("pe", "act", "dve", "pool", "sp")


class Tile:
    __slots__ = ("name", "writers", "readers", "ranges", "alias")
    REG = []

    def __init__(self, name, *ranges):
        self.name = name
        self.writers = []
        self.readers = []
        self.ranges = [(o, o + n) for (o, n) in ranges]
        self.alias = []
        if self.ranges:
            lo = min(a for a, _ in self.ranges)
            hi = max(b for _, b in self.ranges)
            for u in Tile.REG:
                ulo, uhi = u.ranges[0][0], u.ranges[-1][1]
                if ulo >= hi or uhi <= lo:
                    continue
                if any(a < d and c < b for (a, b) in self.ranges for (c, d) in u.ranges):
                    self.alias.append(u)
                    u.alias.append(self)
            self.ranges.sort()
            Tile.REG.append(self)


class Op:
    __slots__ = ("eng", "fn", "idx", "waits", "signal", "dma", "dma_sem", "dma_val", "clock", "rank")


class Prog:
    def __init__(self):
        self.eng_ops = {e: [] for e in ENGS}
        self.seen = {e: {} for e in ENGS}
        self.dma_count = [0] * NDS
        self.dma_rr = {"sp": 0, "pool": NDS_SP}
        self.dma_clock = {}
        self.nops = 0

    def op(self, eng, fn, reads=(), writes=(), dma=False, extra=()):
        o = Op()
        o.eng, o.fn, o.dma, o.signal = eng, fn, dma, False
        deps = list(extra)
        for t in reads:
            deps += t.writers
            for u in t.alias:
                deps += u.writers
        for t in writes:
            deps += t.writers
            deps += t.readers
            for u in t.alias:
                deps += u.writers
                deps += u.readers
        if dma:
            base, cnt = (0, NDS_SP) if eng == "sp" else (NDS_SP, NDS_POOL)
            sem = self.dma_rr[eng]
            self.dma_rr[eng] = base + (sem - base + 1) % cnt
            prev = self.dma_count[sem]
            if prev > 0:
                deps.append(("D", sem, prev))
            self.dma_count[sem] = prev + 16
            o.dma_sem, o.dma_val = sem, prev + 16
        seen = self.seen[eng]
        waits = {}
        for ev in deps:
            if ev[0] == "E":
                _, e2, i2 = ev
                if e2 == eng and eng == "pe":
                    continue
                if seen.get(("E", e2), -1) >= i2:
                    continue
                k = ("E", e2)
                if waits.get(k, -1) < i2:
                    waits[k] = i2
            else:
                _, sem, val = ev
                if seen.get(("D", sem), 0) >= val:
                    continue
                k = ("D", sem)
                if waits.get(k, 0) < val:
                    waits[k] = val
        for k, val in waits.items():
            if k[0] == "E":
                prod = self.eng_ops[k[1]][val]
                prod.signal = True
                clock = prod.clock
            else:
                clock = self.dma_clock[(k[1], val)]
            for kk, vv in clock.items():
                if seen.get(kk, -1) < vv:
                    seen[kk] = vv
            if seen.get(k, -1) < val:
                seen[k] = val
        o.waits = list(waits.items())
        o.idx = len(self.eng_ops[eng])
        o.clock = dict(seen)
        self.eng_ops[eng].append(o)
        self.nops += 1
        if dma:
            ev = ("D", o.dma_sem, o.dma_val)
            self.dma_clock[(o.dma_sem, o.dma_val)] = o.clock
        else:
            ev = ("E", eng, o.idx)
        for t in writes:
            t.writers = [ev]
            t.readers = []
        for t in reads:
            if ev[0] == "E":
                t.readers = [r for r in t.readers if not (r[0] == "E" and r[1] == eng)]
            t.readers.append(ev)
        return o

    def barrier(self):
        deps = []
        for e in ENGS:
            if e != "sp":
                for o in reversed(self.eng_ops[e]):
                    if not o.dma and o.fn is not None:
                        deps.append(("E", e, o.idx))
                        break
        for sem in range(NDS):
            if self.dma_count[sem] > 0:
                deps.append(("D", sem, self.dma_count[sem]))
        b = self.op("sp", lambda e: e.nop(), extra=deps)
        ev = ("E", "sp", b.idx)
        for e in ENGS:
            if e != "sp":
                self.op(e, None, extra=[ev])

    def emit(self, nc):
        for e in ENGS:
            r = 0
            for o in self.eng_ops[e]:
                if o.signal:
                    r += 1
                o.rank = r
        from contextlib import ExitStack
        with ExitStack() as st:
            esem = {e: st.enter_context(nc.semaphore("s_" + e)) for e in ENGS}
            dsem = [st.enter_context(nc.semaphore("d_%d" % i)) for i in range(NDS)]
            block = st.enter_context(nc.Block())

            def run(engname, eng):
                for o in self.eng_ops[engname]:
                    for k, val in o.waits:
                        if k[0] == "E":
                            eng.wait_ge(esem[k[1]], self.eng_ops[k[1]][val].rank)
                        else:
                            eng.wait_ge(dsem[k[1]], val)
                    if o.fn is None:
                        continue
                    ins = o.fn(eng)
                    if o.dma:
                        ins.then_inc(dsem[o.dma_sem], 16)
                    elif o.signal:
                        ins.then_inc(esem[engname], 1)

            @block.tensor
            def _(e):
                run("pe", e)

            @block.scalar
            def _(e):
                run("act", e)

            @block.vector
            def _(e):
                run("dve", e)

            @block.gpsimd
            def _(e):
                run("pool", e)

            @block.sync
            def _(e):
                run("sp", e)


def build_program(T, NSEQ, upto="all"):
    NT = T // 128
    NG = T // 512
    TB = T * 2
    nc = bass.Bass("TRN2", target_bir_lowering=False)
    P = Prog()
    Tile.REG = []

    def din(name, shape, dt=F32):
        return nc.dram_tensor(name, list(shape), dt, kind="ExternalInput").ap()

    x_d = din("x", [NSEQ, T, D])
    w_in_d = din("attn_w_in", [D, INW])
    w_out_d = din("attn_w_out", [D, D])
    pw1_d = din("conv_pw1_w", [D, 2 * D])
    pw2_d = din("conv_pw2_w", [D, D])
    wg_d = din("ffn_w_gate", [2, D, FF])
    wu_d = din("ffn_w_up", [2, D, FF])
    wd_d = din("ffn_w_down", [2, FF, D])
    lng_d = din("ln_g", [4, D])
    lnb_d = din("ln_b", [4, D])
    pw2b_d = din("pw2_b", [1, D])
    cb_d = din("cb", [128, 2048], BF16)
    cf_d = din("cf", [128, 512])
    vec_d = din("vec", [128, 560])
    aug_d = din("aug", [4, 2, 4, T], BF16)
    out_d = nc.dram_tensor("out", [NSEQ, T, D], F32, kind="ExternalOutput").ap()
    if upto != "all":
        dbg_d = nc.dram_tensor("dbg", [128, NT * 1024], F32, kind="ExternalOutput").ap()
        dbgb_d = nc.dram_tensor("dbgb", [128, KC * T], BF16, kind="ExternalOutput").ap()

    ARENA_KB = 200
    arena = nc.alloc_sbuf_tensor("arena", [128, ARENA_KB * 512], BF16)

    def reg(off, nbytes, dt=BF16):
        assert off % 4 == 0 and nbytes % 4 == 0 and off + nbytes <= ARENA_KB * 1024, (off, nbytes)
        a = arena[:, off // 2:(off + nbytes) // 2]
        if dt == F32:
            a = a.bitcast(F32)
        return a

    PSALL = nc.alloc_psum_tensor("psall", [128, 4096], F32)
    PB = [Tile("psb%d" % i) for i in range(8)]

    def psb(i):
        return PSALL[:, i * 512:(i + 1) * 512]

    def psbf(i):
        return PSALL[:, i * 512:(i + 1) * 512].bitcast(BF16)

    def pswide(i):
        return PSALL[:, i * 512:(i + 2) * 512]

    def pspair(i):
        return PSALL[:, i * 512:(i + 2) * 512].rearrange("p (h n) -> p h n", h=2)

    O_CONST = 0
    CB = reg(0, 4096)
    CF = reg(4096, 2048, F32)
    VEC = reg(6144, 2240, F32)
    SM = reg(8384, 1024, F32)
    O_H = 9728
    SZ_H = 64 * 1024
    O_HT = O_H + SZ_H
    SZ_HT = 32 * 1024
    O_O = O_HT + SZ_HT
    SZ_O = 32 * 1024
    O_W = O_O + SZ_O
    SZ_W = 24 * 1024
    O_M = O_W + SZ_W
    SZ_M = 38 * 1024
    assert O_M + SZ_M <= ARENA_KB * 1024

    ident = CB[:, 0:128]
    ones_b = CB[:, 128:256]
    negtinc = CB[:, 256:384]
    negones = CB[:, 384:512]
    masksb = CB[:, 512:1024]
    maskdf = CB[:, 1024:1536]
    zeros_b = CB[:, 1536:2048]
    onesA = CF[:, 0:128]
    onesB = CF[:, 128:256]
    alphaI = CF[:, 256:384]
    identF = CF[:, 384:512]
    V_PW1BV, V_PW1BG, V_DWB, V_CLNG, V_CLNB, V_DWW, V_SUBG, V_LAM = 0, 8, 16, 24, 32, 40, 288, 289
    T_CONST = Tile("const")

    hres = reg(O_H, NT * 4096, F32).rearrange("p (t d) -> p t d", t=NT)
    hT = reg(O_HT, KC * TB).rearrange("p (c t) -> p c t", c=KC)
    HRES = [Tile("hres%d" % i, (O_H + i * 4096, 4096)) for i in range(NT)]
    HT = [Tile("hT%d" % i, *[(O_HT + c * TB + i * 256, 256) for c in range(KC)]) for i in range(NT)]

    P.op("sp", lambda e: e.dma_start(out=CB, in_=cb_d), writes=[T_CONST], dma=True)
    P.op("sp", lambda e: e.dma_start(out=CF, in_=cf_d), writes=[T_CONST], dma=True)
    P.op("sp", lambda e: e.dma_start(out=VEC[:, 0:560], in_=vec_d), writes=[T_CONST], dma=True)

    T_LAM = Tile("lam")
    lamtmp = SM[:, 64:128]

    def lam_ops():
        q1 = VEC[:, V_LAM:V_LAM + 64]
        k1 = VEC[:, V_LAM + 64:V_LAM + 128]
        q2 = VEC[:, V_LAM + 128:V_LAM + 192]
        k2 = VEC[:, V_LAM + 192:V_LAM + 256]
        P.op("dve", lambda e: e.tensor_tensor(out=lamtmp, in0=q1, in1=k1, op=ALU.mult), reads=[T_CONST], writes=[T_LAM])
        P.op("dve", lambda e: e.reduce_sum(out=SM[:, 0:1], in_=lamtmp, axis=mybir.AxisListType.X), reads=[T_LAM], writes=[T_LAM])
        P.op("dve", lambda e: e.tensor_tensor(out=lamtmp, in0=q2, in1=k2, op=ALU.mult), reads=[T_CONST, T_LAM], writes=[T_LAM])
        P.op("dve", lambda e: e.reduce_sum(out=SM[:, 1:2], in_=lamtmp, axis=mybir.AxisListType.X), reads=[T_LAM], writes=[T_LAM])
        P.op("act", lambda e: e.activation(out=SM[:, 2:4], in_=SM[:, 0:2], func=AF.Exp), reads=[T_LAM], writes=[T_LAM])
        P.op("dve", lambda e: e.scalar_tensor_tensor(out=SM[:, 4:5], in0=SM[:, 3:4], scalar=-LAMBDA_INIT, in1=SM[:, 2:3],
                                                     op0=ALU.add, op1=ALU.subtract), reads=[T_LAM], writes=[T_LAM])
        P.op("dve", lambda e: e.tensor_scalar(out=SM[:, 5:6], in0=VEC[:, V_SUBG:V_SUBG + 1], scalar1=(1.0 - LAMBDA_INIT),
                                              scalar2=None, op0=ALU.mult), reads=[T_CONST, T_LAM], writes=[T_LAM])

    lam_ops()
    neglam = SM[:, 4:5]
    gsub = SM[:, 5:6]

    evac_rr = [0]

    def evac_eng():
        evac_rr[0] ^= 1
        return "dve" if evac_rr[0] else "act"

    def copy_op(eng, out, in_, reads, writes, scale=None):
        if eng == "act":
            if scale is None:
                P.op("act", lambda e: e.activation(out=out, in_=in_, func=AF.Copy), reads=reads, writes=writes)
            else:
                P.op("act", lambda e: e.activation(out=out, in_=in_, func=AF.Copy, scale=scale), reads=reads, writes=writes)
        else:
            if scale is None:
                P.op(eng, lambda e: e.tensor_copy(out=out, in_=in_), reads=reads, writes=writes)
            else:
                P.op(eng, lambda e: e.tensor_scalar(out=out, in0=in_, scalar1=scale, scalar2=None, op0=ALU.mult),
                     reads=reads, writes=writes)

    def mm(out, lhsT, rhs, start, stop, reads, writes, skip=False):
        if skip:
            P.op("pe", lambda e: e.matmul(out, lhsT=lhsT, rhs=rhs, start=start, stop=stop, skip_group_check=True),
                 reads=reads, writes=writes)
        else:
            P.op("pe", lambda e: e.matmul(out, lhsT=lhsT, rhs=rhs, start=start, stop=stop), reads=reads, writes=writes)

    LN_Y = [reg(O_M + i * 4096, 4096, F32) for i in range(3)]
    T_LNY = [Tile("lny%d" % i, (O_M + i * 4096, 4096)) for i in range(3)]
    LN_HB = [reg(O_M + 12288 + i * 2048, 2048) for i in range(3)]
    T_LNHB = [Tile("lnhb%d" % i, (O_M + 12288 + i * 2048, 2048)) for i in range(3)]
    LN_G = reg(O_M + 18432, 4096, F32)
    LN_B = reg(O_M + 22528, 4096, F32)
    T_LNGB = Tile("lngb", (O_M + 18432, 8192))
    T_ST = [Tile("lnst%d" % i) for i in range(3)]

    def load_ln_params(idx):
        P.op("sp", lambda e: e.dma_start(out=LN_G, in_=lng_d[idx:idx + 1, :].partition_broadcast(128)), writes=[T_LNGB], dma=True)
        P.op("sp", lambda e: e.dma_start(out=LN_B, in_=lnb_d[idx:idx + 1, :].partition_broadcast(128)), writes=[T_LNGB], dma=True)

    def ln_slots(slot):
        o = 128 + slot * 32
        return (SM[:, o:o + 12], SM[:, o + 12:o + 14], SM[:, o + 14:o + 15], SM[:, o + 15:o + 16], SM[:, o + 16:o + 17])

    def ln_A(tt, slot):
        st, mv, lnv, rstd, nmr = ln_slots(slot)
        Tst = T_ST[slot]
        yv, Ty = LN_Y[slot], T_LNY[slot]
        P.op("dve", lambda e: e.bn_stats(out=st[:, 0:6], in_=yv[:, 0:512]), reads=[Ty], writes=[Tst])
        P.op("dve", lambda e: e.bn_stats(out=st[:, 6:12], in_=yv[:, 512:1024]), reads=[Ty, Tst], writes=[Tst])
        P.op("dve", lambda e: e.bn_aggr(out=mv, in_=st), reads=[Tst], writes=[Tst])
        P.op("dve", lambda e: e.tensor_scalar(out=lnv, in0=mv[:, 0:1], scalar1=-1.0, scalar2=None, op0=ALU.mult), reads=[Tst], writes=[Tst])
        P.op("act", lambda e: e.activation(out=rstd, in_=mv[:, 1:2], func=AF.Ln, bias=EPS_AP), reads=[Tst, T_CONST], writes=[Tst])
        P.op("act", lambda e: e.activation(out=rstd, in_=rstd, func=AF.Exp, scale=-0.5), reads=[Tst], writes=[Tst])
        P.op("act", lambda e: e.activation(out=nmr, in_=lnv, func=AF.Identity, scale=rstd), reads=[Tst], writes=[Tst])

    def ln_B(tt, slot):
        st, mv, lnv, rstd, nmr = ln_slots(slot)
        Tst = T_ST[slot]
        yv, Ty = LN_Y[slot], T_LNY[slot]
        P.op("act", lambda e: e.activation(out=yv, in_=yv, func=AF.Identity, bias=nmr, scale=rstd), reads=[Ty, Tst], writes=[Ty])

    def ln_C(tt, slot):
        yv, Ty = LN_Y[slot], T_LNY[slot]
        P.op("pool", lambda e: e.tensor_tensor(out=yv, in0=yv, in1=LN_G, op=ALU.mult), reads=[Ty, T_LNGB], writes=[Ty])

    def ln_D(s, tt, slot, want_T, out_dram):
        yv, Ty = LN_Y[slot], T_LNY[slot]
        P.op("dve", lambda e: e.tensor_tensor(out=hres[:, tt, :], in0=yv, in1=LN_B, op=ALU.add), reads=[Ty, T_LNGB], writes=[HRES[tt]])
        if out_dram:
            P.op("sp", lambda e: e.dma_start(out=out_d[s, tt * 128:(tt + 1) * 128, :], in_=hres[:, tt, :]), reads=[HRES[tt]], dma=True)
        if want_T:
            hb = LN_HB[slot]
            Thb = T_LNHB[slot]
            P.op("act", lambda e: e.activation(out=hb, in_=hres[:, tt, :], func=AF.Copy), reads=[HRES[tt]], writes=[Thb])

    def ln_E(tt, slot, tbank):
        hb = LN_HB[slot]
        Thb = T_LNHB[slot]
        for c in range(KC):
            P.op("pe", lambda e, c=c: e.transpose(psbf(tbank)[:, c * 128:(c + 1) * 128], hb[:, c * 128:(c + 1) * 128], ident),
                 reads=[Thb, T_CONST], writes=[PB[tbank]])
        P.op("act", lambda e: e.activation(out=hT[:, :, tt * 128:(tt + 1) * 128],
                                           in_=psbf(tbank)[:, 0:1024].rearrange("p (c t) -> p c t", c=KC), func=AF.Copy),
             reads=[PB[tbank]], writes=[HT[tt]])

    def ln_site(s, mm_fn, y_fn, pairs, tbanks, want_T, out_dram, defer=False):
        def stage(step):
            if step < NT:
                ba, bb = pairs[step % 3]
                mm_fn(step, ba, bb)
            t = step - 1
            if 0 <= t < NT:
                ba, bb = pairs[t % 3]
                y_fn(t, t % 3, ba, bb)
                ln_A(t, t % 3)
            t = step - 2
            if 0 <= t < NT:
                ln_B(t, t % 3)
            t = step - 3
            if 0 <= t < NT:
                ln_C(t, t % 3)
                ln_D(s, t, t % 3, want_T, out_dram)
            t = step - 4
            if want_T and 0 <= t < NT:
                ln_E(t, t % 3, tbanks[t % 2])

        for step in range(NT):
            stage(step)
        drain = [(lambda st=st: stage(st)) for st in range(NT, NT + 4)]
        if defer:
            return drain
        for f in drain:
            f()
        return []

    EPS_AP = SM[:, 6:7]
    P.op("pool", lambda e: e.memset(SM[:, 6:7], EPS), writes=[T_CONST])

    xT = reg(O_H, KC * TB).rearrange("p (c t) -> p c t", c=KC)
    XT = [Tile("xT%d" % i, *[(O_H + c * TB + i * 256, 256) for c in range(KC)]) for i in range(NT)]
    Vdf = reg(O_H + KC * TB, NT * 1024).rearrange("p (t c) -> p t c", t=NT)
    VDF = [Tile("vdf%d" % i, (O_H + KC * TB + i * 1024, 1024)) for i in range(NT)]
    O_SETS = O_H + KC * TB + NT * 1024
    SET_SZ = 4 * TB + NT * 512

    class USet:
        pass

    sets = []
    for si in range(2):
        u = USet()
        base = O_SETS + si * SET_SZ
        u.T = [reg(base + j * TB, TB) for j in range(4)]
        u.TT = [[Tile("s%dT%dg%d" % (si, j, g), (base + j * TB + g * 1024, 1024)) for g in range(NG)] for j in range(4)]
        u.Vp = reg(base + 4 * TB, NT * 512).rearrange("p (t c) -> p t c", t=NT)
        u.VP = [Tile("s%dvp%d" % (si, i), (base + 4 * TB + i * 512, 512)) for i in range(NT)]
        u.W = reg(O_W + si * 8192, 8192).rearrange("p (c f) -> p c f", c=KC)
        u.TW = Tile("s%dW" % si, (O_W + si * 8192, 8192))
        sets.append(u)
    assert O_SETS + 2 * SET_SZ <= O_O
    oT = reg(O_O, KC * TB).rearrange("p (c t) -> p c t", c=KC)
    OT = [[Tile("oT%dg%d" % (c, g), (O_O + c * TB + g * 1024, 1024)) for g in range(NG)] for c in range(KC)]
    WV = reg(O_W + 16384, 8192).rearrange("p (c f) -> p c f", c=KC)
    T_WV = Tile("wv", (O_W + 16384, 8192))
    XB = [reg(O_M + 0, 2048), reg(O_M + 2048, 2048), reg(O_M + 29696, 2048), reg(O_M + 31744, 2048)]
    T_XB = [Tile("xb%d" % i, (O_M + (0, 2048, 29696, 31744)[i], 2048)) for i in range(4)]
    EB2 = reg(O_M + 4096, 4096, F32).rearrange("p (h n) -> p h n", h=2)
    T_EB2 = Tile("eb2", (O_M + 4096, 4096))
    SPB2 = [reg(O_M + 8192 + i * 2048, 2048).rearrange("p (h n) -> p h n", h=2) for i in range(2)]
    T_SPB2 = [Tile("spb2_%d" % i, (O_M + 8192 + i * 2048, 2048)) for i in range(2)]
    _sps_off = (O_M + 12288, O_M + 33792)
    SPS2 = [reg(o, 2048).rearrange("p (h n) -> p h n", h=2) for o in _sps_off]
    T_SPS2 = [Tile("sps2_%d" % i, (_sps_off[i], 2048)) for i in range(2)]
    _ab_off = (O_M + 35840, O_M + 25600)
    AB2 = [reg(o, 2048).rearrange("p (h n) -> p h n", h=2) for o in _ab_off]
    T_AB2 = [Tile("ab2_%d" % i, (_ab_off[i], 2048)) for i in range(2)]
    EDF = [reg(O_M + 14336 + i * 1024, 1024) for i in range(3)]
    T_EDF = [Tile("edf%d" % i, (O_M + 14336 + i * 1024, 1024)) for i in range(3)]
    PP = [reg(O_M + 17408 + i * 2048, 2048, F32) for i in range(6)]
    T_PP = [Tile("pp%d" % i, (O_M + 17408 + i * 2048, 2048)) for i in range(6)]
    assert 17408 + 6 * 2048 <= SZ_M

    w_in_v = w_in_d.rearrange("(c p) f -> p c f", p=128)

    def load_unit_weights(uidx, S):
        W, TW = S.W, S.TW
        if uidx < 4:
            qo, ko, vo = uidx * 128, 512 + uidx * 128, 1024 + uidx * 128
            P.op("pool", lambda e: e.dma_start(out=W[:, :, 0:128], in_=w_in_v[:, :, qo:qo + 128]), writes=[TW], dma=True)
            P.op("pool", lambda e: e.dma_start(out=W[:, :, 128:256], in_=w_in_v[:, :, ko:ko + 128]), writes=[TW], dma=True)
            P.op("pool", lambda e: e.dma_start(out=W[:, :, 256:320], in_=w_in_v[:, :, vo:vo + 64]), writes=[TW], dma=True)
            P.op("pool", lambda e: e.dma_start(out=W[:, :, 448:512], in_=w_in_v[:, :, vo + 64:vo + 128]), writes=[TW], dma=True)
        else:
            h = uidx - 4
            qo, ko = 1536 + h * 128, 2048 + h * 128
            P.op("pool", lambda e: e.dma_start(out=W[:, :, 0:128], in_=w_in_v[:, :, qo:qo + 128]), writes=[TW], dma=True)
            P.op("pool", lambda e: e.dma_start(out=W[:, :, 128:256], in_=w_in_v[:, :, ko:ko + 128]), writes=[TW], dma=True)

    def proj_thunks(uidx, S, pbanks):
        th = []
        W, TW = S.W, S.TW
        bank_rr = [0]

        def nextbank():
            b = pbanks[bank_rr[0] % len(pbanks)]
            bank_rr[0] += 1
            return b

        def prep():
            if uidx < 4:
                if uidx < 2:
                    P.op("pool", lambda e: e.memset(S.T[1][64:128, :], 0.0), writes=S.TT[1])
                    P.op("pool", lambda e: e.memset(S.T[2][0:64, :], 0.0), writes=S.TT[2])
            else:
                h = uidx - 4
                if uidx < 6:
                    P.op("pool", lambda e: e.memset(S.T[0][64:128, :], 0.0), writes=S.TT[0])
                    P.op("pool", lambda e: e.memset(S.T[3][0:64, :], 0.0), writes=S.TT[3])
                P.op("sp", lambda e: e.dma_start(out=S.T[1][64:68, :], in_=aug_d[h, 0]), writes=S.TT[1], dma=True)
                P.op("sp", lambda e: e.dma_start(out=S.T[2][0:4, :], in_=aug_d[h, 0]), writes=S.TT[2], dma=True)
                P.op("sp", lambda e: e.dma_start(out=S.T[0][64:68, :], in_=aug_d[h, 1]), writes=S.TT[0], dma=True)
                P.op("sp", lambda e: e.dma_start(out=S.T[3][0:4, :], in_=aug_d[h, 1]), writes=S.TT[3], dma=True)
        th.append(prep)

        def qk(g, which):
            def f():
                b = nextbank()
                co = 0 if which == "q" else 128
                for c in range(KC):
                    mm(psb(b), W[:, c, co:co + 128], xT[:, c, g * 512:(g + 1) * 512], c == 0, c == KC - 1,
                       reads=[TW] + XT[4 * g:4 * g + 4], writes=[PB[b]])
                gs = slice(g * 512, (g + 1) * 512)
                if which == "q":
                    copy_op("dve", S.T[1][0:64, gs], psb(b)[0:64, :], [PB[b]], [S.TT[1][g]], scale=0.125)
                    copy_op("dve", S.T[2][64:128, gs], psb(b)[64:128, :], [PB[b]], [S.TT[2][g]], scale=0.125)
                else:
                    if uidx < 4:
                        copy_op("dve", S.T[0][:, gs], psb(b), [PB[b]], [S.TT[0][g]])
                    else:
                        copy_op("dve", S.T[0][0:64, gs], psb(b)[0:64, :], [PB[b]], [S.TT[0][g]])
                        copy_op("dve", S.T[3][64:128, gs], psb(b)[64:128, :], [PB[b]], [S.TT[3][g]])
            return f

        def vproj(tt):
            def f():
                b = nextbank()
                for c in range(KC):
                    mm(psb(b)[:, 0:256], xT[:, c, tt * 128:(tt + 1) * 128], W[:, c, 256:512], c == 0, c == KC - 1,
                       reads=[TW, XT[tt]], writes=[PB[b]])
                copy_op("dve", S.Vp[:, tt, :], psb(b)[:, 0:256], [PB[b]], [S.VP[tt]])
            return f

        for g in range(NG):
            th.append(qk(g, "k"))
        for g in range(NG):
            th.append(qk(g, "q"))
        if uidx < 4:
            for tt in range(NT):
                th.append(vproj(tt))
        return th

    def sb_items(uidx, S):
        items = []
        for g in range(NG):
            kmax = min(4 * g + 3, NT - 1)
            for kb in range(kmax, -1, -1):
                items.append((g, kb, kb == kmax, kb == 0))
        n = len(items)
        ZP, DP = 0, 2
        OB = (4, 5)
        cur_of = []
        cnt = 0
        for (g, kb, first, last) in items:
            if first:
                cnt = 0
            cur_of.append(cnt % 2)
            if kb > 0:
                cnt += 1
        Qs = (S.T[1], S.T[2])
        QTs = (S.TT[1], S.TT[2])

        def geom(i):
            g, kb, first, last = items[i]
            c0 = max(0, kb - 4 * g) * 128
            diag = kb >= 4 * g
            kt = S.T[0][:, kb * 128:(kb + 1) * 128]
            return g, kb, first, last, c0, diag, kt

        def Zst(i):
            g, kb, first, last, c0, diag, kt = geom(i)
            for h in range(2):
                bnk = ZP + h
                mm(psb(bnk)[:, c0:512], kt, Qs[h][:, g * 512 + c0:(g + 1) * 512], True, True,
                   reads=[S.TT[0][kb // 4], QTs[h][g]], writes=[PB[bnk]])
                if diag:
                    mm(psb(bnk)[:, c0:c0 + 128], ident, masksb[:, 0:128], False, True, reads=[T_CONST], writes=[PB[bnk]], skip=True)

        def ESst(i):
            g, kb, first, last, c0, diag, kt = geom(i)
            spb, Tsp = SPB2[i % 2], T_SPB2[i % 2]
            P.op("act", lambda e: e.activation(out=EB2[:, :, c0:512], in_=pspair(ZP)[:, :, c0:512], func=AF.Exp),
                 reads=[PB[ZP], PB[ZP + 1]], writes=[T_EB2])
            P.op("act", lambda e: e.activation(out=spb[:, :, c0:512], in_=EB2[:, :, c0:512], func=AF.Ln, bias=1.0),
                 reads=[T_EB2], writes=[Tsp])

        def DDst(i):
            g, kb, first, last, c0, diag, kt = geom(i)
            spb, Tsp = SPB2[i % 2], T_SPB2[i % 2]
            cur = cur_of[i]
            if first:
                P.op("pool", lambda e: e.memset(SPS2[0], 0.0), writes=[T_SPS2[0]])
                P.op("pool", lambda e: e.memset(SPS2[1], 0.0), writes=[T_SPS2[1]])
            for h in range(2):
                bnk = DP + h
                mm(psb(bnk)[:, c0:512], kt, Qs[h][:, g * 512 + c0:(g + 1) * 512], True, False,
                   reads=[S.TT[0][kb // 4], QTs[h][g]], writes=[PB[bnk]])
                mm(psb(bnk)[:, c0:512], negtinc, spb[:, h, c0:512], False, first, reads=[T_CONST, Tsp], writes=[PB[bnk]])
                if not first:
                    mm(psb(bnk)[:, c0:512], negones, SPS2[cur][:, h, c0:512], False, True, reads=[T_CONST, T_SPS2[cur]], writes=[PB[bnk]])
                if diag:
                    mm(psb(bnk)[:, c0:c0 + 128], ident, masksb[:, 0:128], False, True, reads=[T_CONST], writes=[PB[bnk]], skip=True)

        def AAst(i):
            g, kb, first, last, c0, diag, kt = geom(i)
            ab, Ta = AB2[i % 2], T_AB2[i % 2]
            P.op("act", lambda e: e.activation(out=ab[:, :, c0:512], in_=pspair(DP)[:, :, c0:512], func=AF.Exp),
                 reads=[PB[DP], PB[DP + 1]], writes=[Ta])

        def SUst(i):
            g, kb, first, last, c0, diag, kt = geom(i)
            if kb > 0:
                spb, Tsp = SPB2[i % 2], T_SPB2[i % 2]
                cur = cur_of[i]
                nxt = 1 - cur
                P.op("dve", lambda e: e.tensor_tensor(out=SPS2[nxt][:, :, c0:512], in0=SPS2[cur][:, :, c0:512], in1=spb[:, :, c0:512], op=ALU.add),
                     reads=[T_SPS2[cur], Tsp], writes=[T_SPS2[nxt]])

        def AVst(i):
            g, kb, first, last, c0, diag, kt = geom(i)
            ob = OB[g % 2]
            ab, Ta = AB2[i % 2], T_AB2[i % 2]
            for h in range(2):
                vp = S.Vp[:, kb, h * 128:(h + 1) * 128]
                mm(psb(ob)[:, c0:512], vp, ab[:, h, c0:512], first and h == 0, last and h == 1, reads=[S.VP[kb], Ta], writes=[PB[ob]], skip=True)
            if last:
                copy_op("dve", oT[:, uidx, g * 512:(g + 1) * 512], psb(ob), [PB[ob]], [OT[uidx][g]])

        th = []

        def pro():
            Zst(0)
            ESst(0)
            if n > 1:
                Zst(1)
        th.append(pro)
        for r in range(n):
            def f(r=r):
                DDst(r)
                if r >= 1:
                    AVst(r - 1)
                if r + 1 < n:
                    ESst(r + 1)
                if r + 2 < n:
                    Zst(r + 2)
                AAst(r)
                SUst(r)
            th.append(f)
        th.append(lambda: AVst(n - 1))
        return th

    ONE_AP = SM[:, 7:8]
    P.op("pool", lambda e: e.memset(SM[:, 7:8], 1.0), writes=[T_CONST])

    DF_PENDING = []

    def df_items(uidx, S):
        h = uidx - 4
        c8 = 4 + h
        items = []
        for g in range(NG):
            kmax = min(4 * g + 3, NT - 1)
            for m in range(2):
                for kb in range(kmax, -1, -1):
                    items.append((g, m, kb, m == 0 and kb == kmax, m == 1 and kb == 0))
        SBK = (0, 1, 6)
        OBK = (2, 3)
        SMK = (4, 5)
        MSB = 7

        def stageA(i):
            g, m, kb, gfirst, glast = items[i]
            sbk = SBK[i % 3]
            c0 = max(0, kb - 4 * g) * 128
            diag = kb >= 4 * g
            K, KTl = (S.T[0], S.TT[0]) if m == 0 else (S.T[3], S.TT[3])
            Q, QTl = (S.T[1], S.TT[1]) if m == 0 else (S.T[2], S.TT[2])
            mm(psb(sbk)[:, c0:512], K[:, kb * 128:(kb + 1) * 128], Q[:, g * 512 + c0:(g + 1) * 512], True, True,
               reads=[KTl[kb // 4], QTl[g]], writes=[PB[sbk]])
            if diag:
                mm(psb(sbk)[:, c0:c0 + 128], ident, maskdf[:, 0:128], False, True, reads=[T_CONST], writes=[PB[sbk]], skip=True)

        def stageE(i):
            g, m, kb, gfirst, glast = items[i]
            sbk = SBK[i % 3]
            c0 = max(0, kb - 4 * g) * 128
            eb, Te = EDF[i % 3], T_EDF[i % 3]
            P.op("act", lambda e: e.activation(out=eb[:, c0:512], in_=psb(sbk)[:, c0:512], func=AF.Exp), reads=[PB[sbk]], writes=[Te])

        def stageB(i):
            g, m, kb, gfirst, glast = items[i]
            c0 = max(0, kb - 4 * g) * 128
            eb, Te = EDF[i % 3], T_EDF[i % 3]
            kfirst = kb == min(4 * g + 3, NT - 1)
            ob, sb_ = OBK[m], SMK[m]
            mm(psb(ob)[:, c0:512], Vdf[:, kb, h * 128:(h + 1) * 128], eb[:, c0:512], kfirst, kb == 0, reads=[VDF[kb], Te], writes=[PB[ob]], skip=True)
            mm(psb(sb_)[:, c0:512], ones_b, eb[:, c0:512], kfirst, kb == 0, reads=[T_CONST, Te], writes=[PB[sb_]], skip=True)
            if glast:
                post(g)

        pending = DF_PENDING

        def post(g):
            r0, r1, t0, t1, o, sq = PP
            Tr0, Tr1, Tt0, Tt1, To, Tsq = T_PP
            P.op("act", lambda e: e.activation(out=r0, in_=psb(SMK[0]), func=AF.Ln), reads=[PB[SMK[0]]], writes=[Tr0])
            P.op("dve", lambda e: e.tensor_copy(out=t0, in_=psb(OBK[0])), reads=[PB[OBK[0]]], writes=[Tt0])
            P.op("act", lambda e: e.activation(out=r1, in_=psb(SMK[1]), func=AF.Ln), reads=[PB[SMK[1]]], writes=[Tr1])
            P.op("dve", lambda e: e.tensor_copy(out=t1, in_=psb(OBK[1])), reads=[PB[OBK[1]]], writes=[Tt1])

            def st1():
                P.op("act", lambda e: e.activation(out=r0, in_=r0, func=AF.Exp, scale=-1.0), reads=[Tr0], writes=[Tr0])
                P.op("act", lambda e: e.activation(out=r1, in_=r1, func=AF.Exp, scale=-1.0), reads=[Tr1], writes=[Tr1])
                pending.append([1, st2])

            def st2():
                P.op("dve", lambda e: e.tensor_tensor(out=t0, in0=t0, in1=r0, op=ALU.mult), reads=[Tt0, Tr0], writes=[Tt0])
                P.op("dve", lambda e: e.tensor_tensor(out=t1, in0=t1, in1=r1, op=ALU.mult), reads=[Tt1, Tr1], writes=[Tt1])
                P.op("dve", lambda e: e.scalar_tensor_tensor(out=o, in0=t1, scalar=neglam, in1=t0, op0=ALU.mult, op1=ALU.add),
                     reads=[Tt0, Tt1, T_LAM], writes=[To])
                pending.append([2, st3])

            def st3():
                P.op("act", lambda e: e.activation(out=sq, in_=o, func=AF.Square), reads=[To], writes=[Tsq])
                pending.append([1, st4])

            def st4():
                mm(psb(MSB), onesA, sq, True, True, reads=[T_CONST, Tsq], writes=[PB[MSB]])
                P.op("act", lambda e: e.activation(out=r0, in_=psb(MSB), func=AF.Ln, bias=EPS_AP), reads=[PB[MSB], T_CONST], writes=[Tr0])
                pending.append([1, st5])

            def st5():
                P.op("act", lambda e: e.activation(out=r1, in_=r0, func=AF.Exp, scale=-0.5), reads=[Tr0], writes=[Tr1])
                pending.append([1, st6])

            def st6():
                P.op("dve", lambda e: e.scalar_tensor_tensor(out=oT[:, c8, g * 512:(g + 1) * 512], in0=o, scalar=gsub, in1=r1,
                                                             op0=ALU.mult, op1=ALU.mult), reads=[To, Tr1, T_LAM], writes=[OT[c8][g]])

            pending.append([1, st1])

        def run_pending(force=False):
            if force:
                while pending:
                    pending.pop(0)[1]()
                return
            for it in list(pending):
                it[0] -= 1
                if it[0] <= 0:
                    pending.remove(it)
                    it[1]()

        th = []
        n = len(items)

        def pro():
            stageA(0)
            if n > 1:
                stageA(1)
        th.append(pro)
        for r in range(n):
            def f(r=r):
                if r + 2 < n:
                    stageA(r + 2)
                stageE(r)
                run_pending()
                if r >= 1:
                    stageB(r - 1)
            th.append(f)

        def epi():
            stageB(n - 1)
            if uidx == 7:
                run_pending(force=True)
        th.append(epi)
        return th

    def run_interleaved(main, bg):
        nb = len(bg)
        nm = max(1, len(main))
        done = 0
        for i, f in enumerate(main):
            f()
            want = ((i + 1) * nb + nm - 1) // nm
            while done < min(want, nb):
                bg[done]()
                done += 1
        while done < nb:
            bg[done]()
            done += 1

    NQ = [6, 6, 5, 5]
    aT = reg(O_O, 6 * TB).rearrange("p (c t) -> p c t", c=6)
    AT = [[Tile("aT%dt%d" % (j, tt), (O_O + j * TB + tt * 256, 256)) for tt in range(NT)] for j in range(6)]
    WDQ = [reg(O_W + i * 12288, 12288).rearrange("p (j n) -> p j n", j=6) for i in range(2)]
    T_WDQ = [Tile("wdq%d" % i, (O_W + i * 12288, 12288)) for i in range(2)]
    WGU = [reg(O_O + 6 * TB, 8192).rearrange("p (w c f) -> p w c f", w=2, c=KC),
           reg(O_M + 26624, 8192).rearrange("p (w c f) -> p w c f", w=2, c=KC)]
    T_WGU = [Tile("wgu0", (O_O + 6 * TB, 8192)), Tile("wgu1", (O_M + 26624, 8192))]
    SG = [reg(O_M + 34816, 2048, F32), reg(O_M + 36864, 2048, F32)]
    T_SG = [Tile("sg%d" % i, (O_M + 34816 + i * 2048, 2048)) for i in range(2)]
    assert 6 * TB + 8192 <= SZ_O and 38912 <= SZ_M

    def ffn_prefetch(layer):
        wg_v = wg_d[layer].rearrange("(c p) f -> p c f", p=128)
        wu_v = wu_d[layer].rearrange("(c p) f -> p c f", p=128)
        for bi in range(2):
            col0 = bi * 256
            P.op("pool", lambda e, bi=bi, col0=col0: e.dma_start(out=WGU[bi][:, 0, :, 0:256], in_=wg_v[:, :, col0:col0 + 256]),
                 writes=[T_WGU[bi]], dma=True)
            P.op("pool", lambda e, bi=bi, col0=col0: e.dma_start(out=WGU[bi][:, 1, :, 0:256], in_=wu_v[:, :, col0:col0 + 256]),
                 writes=[T_WGU[bi]], dma=True)

    def ffn(s, layer, ln_idx, last, prefetched=False, pre=None):
        wg_v = wg_d[layer].rearrange("(c p) f -> p c f", p=128)
        wu_v = wu_d[layer].rearrange("(c p) f -> p c f", p=128)
        wd_v = wd_d[layer].rearrange("(j p) n -> p j n", p=128)
        pre = list(pre or [])
        if pre:
            pre.pop(0)()
        else:
            load_ln_params(ln_idx)
        fblocks = []
        c_abs = 0
        qstart = []
        for q in range(4):
            qstart.append(c_abs)
            j = 0
            while j < NQ[q]:
                n = min(2, NQ[q] - j)
                fblocks.append((q, j, n, (c_abs + j) * 128))
                j += n
            c_abs += NQ[q]
        blk_i = [0]

        def load_block(bi):
            q, j0, n, col0 = fblocks[bi]
            buf = bi % 2
            P.op("pool", lambda e: e.dma_start(out=WGU[buf][:, 0, :, 0:n * 128], in_=wg_v[:, :, col0:col0 + n * 128]),
                 writes=[T_WGU[buf]], dma=True)
            P.op("pool", lambda e: e.dma_start(out=WGU[buf][:, 1, :, 0:n * 128], in_=wu_v[:, :, col0:col0 + n * 128]),
                 writes=[T_WGU[buf]], dma=True)

        def load_wd(q):
            buf = q % 2
            nq = NQ[q]
            P.op("pool", lambda e: e.dma_start(out=WDQ[buf][:, 0:nq, :], in_=wd_v[:, qstart[q]:qstart[q] + nq, :]),
                 writes=[T_WDQ[buf]], dma=True)

        if not prefetched:
            load_block(0)
            load_block(1)
        load_wd(0)
        gu_rr = [0]

        def gate_up(bi, jj, j, g):
            buf = bi % 2
            gb, ub = (0, 1) if gu_rr[0] % 2 == 0 else (2, 3)
            gu_rr[0] += 1
            for w_, bnk in ((0, gb), (1, ub)):
                for c in range(KC):
                    mm(psb(bnk), WGU[buf][:, w_, c, jj * 128:(jj + 1) * 128], hT[:, c, g * 512:(g + 1) * 512],
                       c == 0, c == KC - 1, reads=[T_WGU[buf]] + HT[4 * g:4 * g + 4], writes=[PB[bnk]])
            sgb, Tsg = SG[gu_rr[0] % 2], T_SG[gu_rr[0] % 2]
            P.op("act", lambda e: e.activation(out=sgb, in_=psb(gb), func=AF.Silu), reads=[PB[gb]], writes=[Tsg])
            P.op("dve", lambda e: e.tensor_tensor(out=aT[:, j, g * 512:(g + 1) * 512], in0=psb(ub), in1=sgb, op=ALU.mult),
                 reads=[PB[ub], Tsg], writes=AT[j][4 * g:4 * g + 4])

        for q in range(4):
            if q + 1 < 4:
                load_wd(q + 1)
            for bi, (bq, j0, n, col0) in enumerate(fblocks):
                if bq != q:
                    continue
                if pre:
                    for g in range(NG - 1):
                        for jj in range(n):
                            gate_up(bi, jj, j0 + jj, g)
                        if pre:
                            pre.pop(0)()
                    while pre:
                        pre.pop(0)()
                    load_ln_params(ln_idx)
                    for jj in range(n):
                        gate_up(bi, jj, j0 + jj, NG - 1)
                else:
                    for jj in range(n):
                        for g in range(NG):
                            gate_up(bi, jj, j0 + jj, g)
                if bi + 2 < len(fblocks):
                    load_block(bi + 2)
            nq = NQ[q]
            wbuf = q % 2

            def down_mm(tt, ba, bb, nq=nq, wbuf=wbuf, q=q):
                for half, bnk in ((0, ba), (1, bb)):
                    for j in range(nq):
                        mm(psb(bnk), aT[:, j, tt * 128:(tt + 1) * 128], WDQ[wbuf][:, j, half * 512:(half + 1) * 512],
                           j == 0, j == nq - 1, reads=[AT[j][tt], T_WDQ[wbuf]], writes=[PB[bnk]])

            if q < 3:
                for tt in range(NT):
                    ba, bb = (4, 5) if tt % 2 == 0 else (6, 7)
                    down_mm(tt, ba, bb)
                    for half, bnk in ((0, ba), (1, bb)):
                        hs = hres[:, tt, half * 512:(half + 1) * 512]
                        if q == 0:
                            P.op("dve", lambda e, hs=hs, bnk=bnk: e.scalar_tensor_tensor(out=hs, in0=hs, scalar=ALPHA, in1=psb(bnk), op0=ALU.mult, op1=ALU.add),
                                 reads=[HRES[tt], PB[bnk]], writes=[HRES[tt]])
                        else:
                            P.op("dve", lambda e, hs=hs, bnk=bnk: e.tensor_tensor(out=hs, in0=hs, in1=psb(bnk), op=ALU.add),
                                 reads=[HRES[tt], PB[bnk]], writes=[HRES[tt]])
            else:
                def y_fn(tt, slot, ba, bb):
                    yv, Ty = LN_Y[slot], T_LNY[slot]
                    assert bb == ba + 1
                    P.op("dve", lambda e: e.tensor_tensor(out=yv, in0=hres[:, tt, :], in1=pswide(ba), op=ALU.add),
                         reads=[HRES[tt], PB[ba], PB[bb]], writes=[Ty])
                ln_site(s, down_mm, y_fn, ((2, 3), (4, 5), (6, 7)), (0, 1), want_T=not last, out_dram=last)

    def layer0(s):
        def xload(tt):
            xb, Txb = XB[tt % 4], T_XB[tt % 4]
            P.op("pool", lambda e, xb=xb, tt=tt: e.dma_start(out=xb, in_=x_d[s, tt * 128:(tt + 1) * 128, :]), writes=[Txb], dma=True)

        def vdf_proj(tt):
            b = 2 + tt % 2
            for c in range(KC):
                mm(psb(b), xT[:, c, tt * 128:(tt + 1) * 128], WV[:, c, :], c == 0, c == KC - 1, reads=[T_WV, XT[tt]], writes=[PB[b]])
            copy_op(evac_eng(), Vdf[:, tt, :], psb(b), [PB[b]], [VDF[tt]])

        for tt in range(min(4, NT)):
            xload(tt)
        P.op("pool", lambda e: e.dma_start(out=WV, in_=w_in_v[:, :, 2560:3072]), writes=[T_WV], dma=True)
        for S_ in sets:
            P.op("pool", lambda e, S_=S_: e.memset(S_.W[:, :, 320:448], 0.0), writes=[S_.TW])
        load_unit_weights(0, sets[0])
        for tt in range(NT):
            xb, Txb = XB[tt % 4], T_XB[tt % 4]
            if tt + 4 < NT:
                pass
            b = tt % 2
            for c in range(KC):
                P.op("pe", lambda e, c=c, xb=xb, b=b: e.transpose(psbf(b)[:, c * 128:(c + 1) * 128], xb[:, c * 128:(c + 1) * 128], ident),
                     reads=[Txb, T_CONST], writes=[PB[b]])
            eng = evac_eng()
            copy_op(eng, xT[:, :, tt * 128:(tt + 1) * 128], psbf(b)[:, 0:1024].rearrange("p (c t) -> p c t", c=KC), [PB[b]], [XT[tt]])
            if tt + 4 < NT:
                xload(tt + 4)
            if tt >= 2:
                vdf_proj(tt - 2)
        load_unit_weights(1, sets[1])
        for tt in range(max(0, NT - 2), NT):
            vdf_proj(tt)
        for f in proj_thunks(0, sets[0], (6, 7)):
            f()
        for u in range(8):
            S = sets[u % 2]
            if u + 1 < 8:
                bg = proj_thunks(u + 1, sets[(u + 1) % 2], (6, 7) if (u < 4) else (7,))
            else:
                bg = []
            main = sb_items(u, S) if u < 4 else df_items(u, S)
            if u == 7:
                P.op("pool", lambda e: e.dma_start(out=WOUT, in_=w_out_d.rearrange("(c p) n -> p c n", p=128)), writes=[T_WOUT], dma=True)
            run_interleaved(main, bg)
            if u + 2 < 8:
                load_unit_weights(u + 2, S)

    XRES = [reg(O_M + 26624, 4096, F32), reg(O_M + 30720, 4096, F32), reg(O_W + 16384, 4096, F32)]
    T_XRES = [Tile("xres0", (O_M + 26624, 4096)), Tile("xres1", (O_M + 30720, 4096)), Tile("xres2", (O_W + 16384, 4096))]
    WOUT = reg(O_W, 16384).rearrange("p (c n) -> p c n", c=KC)
    T_WOUT = Tile("wout", (O_W, 16384))

    def wout_ln1(s):
        load_ln_params(0)

        def mm_fn(tt, ba, bb):
            xr, Txr = XRES[tt % 3], T_XRES[tt % 3]
            P.op("sp", lambda e, xr=xr, tt=tt: e.dma_start(out=xr, in_=x_d[s, tt * 128:(tt + 1) * 128, :]), writes=[Txr], dma=True)
            for half, bnk in ((0, ba), (1, bb)):
                for c in range(KC):
                    mm(psb(bnk), oT[:, c, tt * 128:(tt + 1) * 128], WOUT[:, c, half * 512:(half + 1) * 512], c == 0, c == KC - 1,
                       reads=[OT[c][tt // 4], T_WOUT], writes=[PB[bnk]])

        def y_fn(tt, slot, ba, bb):
            xr, Txr = XRES[tt % 3], T_XRES[tt % 3]
            yv, Ty = LN_Y[slot], T_LNY[slot]
            assert bb == ba + 1
            P.op("dve", lambda e: e.scalar_tensor_tensor(out=yv, in0=xr, scalar=ALPHA, in1=pswide(ba), op0=ALU.mult, op1=ALU.add),
                 reads=[Txr, PB[ba], PB[bb]], writes=[Ty])

        return ln_site(s, mm_fn, y_fn, ((0, 1), (2, 3), (4, 5)), (6, 7), want_T=True, out_dram=False, defer=True)

    cvT = reg(O_O, 16384, F32).rearrange("p (c t) -> p c t", c=KC)
    CVT = [Tile("cv%d" % c, (O_O + c * 2048, 2048)) for c in range(KC)]
    UW = 544
    uT = reg(O_O + 16384, KC * UW * 2).rearrange("p (c t) -> p c t", c=KC)
    UT = [Tile("uT%d" % c, (O_O + 16384 + c * UW * 2, UW * 2)) for c in range(KC)]
    DG = [reg(O_W + i * 8192, 31 * 256).rearrange("p (k j) -> p k j", k=31) for i in range(2)]
    T_DG = [Tile("dg%d" % i, (O_W + i * 8192, 31 * 256)) for i in range(2)]
    _w1_off = (O_W + 16384, O_W + 20480, O_O + 16384 + KC * UW * 2)
    assert _w1_off[2] + 4096 <= O_O + SZ_O
    W1B = [reg(o, 4096).rearrange("p (c f) -> p c f", c=KC) for o in _w1_off]
    T_W1B = [Tile("w1b%d" % i, (_w1_off[i], 4096)) for i in range(3)]
    SIG = [reg(O_M + 20480, 2048, F32), reg(O_M + 22528, 2048, F32)]
    T_SIG = [Tile("sig%d" % i, (O_M + 20480 + i * 2048, 2048)) for i in range(2)]
    SQ = [reg(O_M + 24576, 2048, F32), reg(O_M + 26624, 2048, F32)]
    T_SQ = [Tile("sq%d" % i, (O_M + 24576 + i * 2048, 2048)) for i in range(2)]
    RSTD = reg(O_M + 28672, 2048, F32)
    T_RSTD = Tile("rstdc", (O_M + 28672, 2048))
    LNT = reg(O_M + 30720, 2048, F32)
    T_LNT = Tile("lnt", (O_M + 30720, 2048))
    DWB16 = reg(O_M + 32768, 512)
    ACCM = reg(O_M + 33792, 2048, F32)
    ACCV = reg(O_M + 35840, 2048, F32)
    T_ACCM = Tile("caccm", (O_M + 33792, 2048))
    T_ACCV = Tile("caccv", (O_M + 35840, 2048))
    T_DWB = Tile("dwb16", (O_M + 32768, 512))
    assert 16384 + KC * UW * 2 <= SZ_O
    pw1_v = pw1_d.rearrange("(c p) f -> p c f", p=128)

    def conv_mixer(s):
        P.op("dve", lambda e: e.tensor_copy(out=DWB16[:, 0:248], in_=VEC[:, V_DWW:V_DWW + 248]), reads=[T_CONST], writes=[T_DWB])
        seq = [(g, c) for g in range(NG) for c in range(KC)]
        N = len(seq)

        def load_w1(i):
            g, c = seq[i]
            buf = i % 3
            P.op("pool", lambda e: e.dma_start(out=W1B[buf][:, :, 0:128], in_=pw1_v[:, :, c * 128:(c + 1) * 128]), writes=[T_W1B[buf]], dma=True)
            P.op("pool", lambda e: e.dma_start(out=W1B[buf][:, :, 128:256], in_=pw1_v[:, :, D + c * 128:D + (c + 1) * 128]), writes=[T_W1B[buf]], dma=True)

        def build_diag(i):
            g, c = seq[i]
            dg, Tdg = DG[i % 2], T_DG[i % 2]
            P.op("dve", lambda e: e.tensor_tensor(
                out=dg, in0=ident.rearrange("p (o j) -> p o j", o=1).broadcast_to([128, 31, 128]),
                in1=DWB16[:, c * 31:(c + 1) * 31].rearrange("p (k o) -> p k o", o=1).broadcast_to([128, 31, 128]), op=ALU.mult),
                reads=[T_CONST, T_DWB], writes=[Tdg])

        def banks(i):
            return (0, 1) if i % 2 == 0 else (2, 3)

        def Pst(i):
            g, c = seq[i]
            buf = i % 3
            vb, gbk = banks(i)
            for co, bnk in ((0, vb), (128, gbk)):
                for k in range(KC):
                    mm(psb(bnk), W1B[buf][:, k, co:co + 128], hT[:, k, g * 512:(g + 1) * 512], k == 0, k == KC - 1,
                       reads=[T_W1B[buf]] + HT[4 * g:4 * g + 4], writes=[PB[bnk]])
            if i + 3 < N:
                load_w1(i + 3)

        def Est(i):
            g, c = seq[i]
            vb, gbk = banks(i)
            sg, Tsg = SIG[i % 2], T_SIG[i % 2]
            P.op("act", lambda e: e.activation(out=sg, in_=psb(gbk), func=AF.Sigmoid, bias=VEC[:, V_PW1BG + c:V_PW1BG + c + 1]),
                 reads=[PB[gbk], T_CONST], writes=[Tsg])
            if g == 0:
                P.op("dve", lambda e: e.memset(uT[:, c, 0:32], 0.0), writes=[UT[c]])
            else:
                P.op("dve", lambda e: e.tensor_copy(out=uT[:, c, 0:32], in_=uT[:, c, 512:544]), reads=[UT[c]], writes=[UT[c]])
            P.op("dve", lambda e: e.scalar_tensor_tensor(out=uT[:, c, 32:544], in0=psb(vb), scalar=VEC[:, V_PW1BV + c:V_PW1BV + c + 1],
                                                         in1=sg, op0=ALU.add, op1=ALU.mult),
                 reads=[PB[vb], Tsg, T_CONST, UT[c]], writes=[UT[c]])
            if i + 1 < N:
                build_diag(i + 1)

        MB, VB = 6, 7

        def Cmm(i):
            g, c = seq[i]
            dg, Tdg = DG[i % 2], T_DG[i % 2]
            cb_ = 4 + i % 2
            for k in range(31):
                mm(psb(cb_), dg[:, k, :], uT[:, c, 2 + k:2 + k + 512], k == 0, k == 30, reads=[Tdg, UT[c]], writes=[PB[cb_]])

        def Cev(i):
            g, c = seq[i]
            cb_ = 4 + i % 2
            P.op("act", lambda e: e.activation(out=cvT[:, c, :], in_=psb(cb_), func=AF.Identity, bias=VEC[:, V_DWB + c:V_DWB + c + 1]),
                 reads=[PB[cb_], T_CONST], writes=[CVT[c]])
            sq, Tsq = SQ[c % 2], T_SQ[c % 2]
            P.op("act", lambda e: e.activation(out=sq, in_=cvT[:, c, :], func=AF.Square), reads=[CVT[c]], writes=[Tsq])
            if c == 1:
                P.op("dve", lambda e: e.tensor_tensor(out=ACCM, in0=cvT[:, 0, :], in1=cvT[:, 1, :], op=ALU.add),
                     reads=[CVT[0], CVT[1]], writes=[T_ACCM])
                P.op("dve", lambda e: e.tensor_tensor(out=ACCV, in0=SQ[0], in1=SQ[1], op=ALU.add),
                     reads=[T_SQ[0], T_SQ[1]], writes=[T_ACCV])
            elif c >= 2:
                P.op("dve", lambda e: e.tensor_tensor(out=ACCM, in0=ACCM, in1=cvT[:, c, :], op=ALU.add),
                     reads=[T_ACCM, CVT[c]], writes=[T_ACCM])
                P.op("dve", lambda e: e.tensor_tensor(out=ACCV, in0=ACCV, in1=sq, op=ALU.add),
                     reads=[T_ACCV, Tsq], writes=[T_ACCV])

        def stats_mm(i):
            mm(psb(MB), onesB, ACCM, True, True, reads=[T_CONST, T_ACCM], writes=[PB[MB]])
            mm(psb(VB), onesB, ACCV, True, True, reads=[T_CONST, T_ACCV], writes=[PB[VB]])

        def LNfin(g):
            P.op("act", lambda e: e.activation(out=LNT, in_=psb(MB), func=AF.Copy), reads=[PB[MB]], writes=[T_LNT])
            P.op("dve", lambda e: e.tensor_tensor(out=RSTD, in0=LNT, in1=LNT, op=ALU.mult), reads=[T_LNT], writes=[T_RSTD])
            P.op("dve", lambda e: e.tensor_tensor(out=RSTD, in0=psb(VB), in1=RSTD, op=ALU.subtract), reads=[PB[VB], T_RSTD], writes=[T_RSTD])
            P.op("act", lambda e: e.activation(out=RSTD, in_=RSTD, func=AF.Ln, bias=EPS_AP), reads=[T_RSTD, T_CONST], writes=[T_RSTD])
            P.op("act", lambda e: e.activation(out=RSTD, in_=RSTD, func=AF.Exp, scale=-0.5), reads=[T_RSTD], writes=[T_RSTD])

        def LNchunk(g, c):
            P.op("dve", lambda e: e.tensor_tensor(out=cvT[:, c, :], in0=cvT[:, c, :], in1=LNT, op=ALU.subtract),
                 reads=[CVT[c], T_LNT], writes=[CVT[c]])
            P.op("dve", lambda e: e.scalar_tensor_tensor(out=cvT[:, c, :], in0=cvT[:, c, :], scalar=VEC[:, V_CLNG + c:V_CLNG + c + 1],
                                                         in1=RSTD, op0=ALU.mult, op1=ALU.mult),
                 reads=[CVT[c], T_RSTD, T_CONST], writes=[CVT[c]])
            P.op("act", lambda e: e.activation(out=hT[:, c, g * 512:(g + 1) * 512], in_=cvT[:, c, :], func=AF.Silu,
                                               bias=VEC[:, V_CLNB + c:V_CLNB + c + 1]),
                 reads=[CVT[c], T_CONST], writes=HT[4 * g:4 * g + 4])

        load_w1(0)
        load_w1(1)
        if N > 2:
            load_w1(2)
        build_diag(0)
        Pst(0)
        done_P = {0}
        done_E = set()
        lnq = []
        for i in range(N):
            g, c = seq[i]
            if i + 1 < N and (i + 1) not in done_P:
                Pst(i + 1)
                done_P.add(i + 1)
            if i not in done_E:
                Est(i)
                done_E.add(i)
            for _ in range(2):
                if lnq:
                    lnq.pop(0)()
            Cmm(i)
            Cev(i)
            if c == KC - 1:
                if i + 2 < N:
                    Pst(i + 2)
                    done_P.add(i + 2)
                stats_mm(i)
                if i + 1 < N:
                    Est(i + 1)
                    done_E.add(i + 1)
                LNfin(g)
                for cc in range(KC):
                    lnq.append(lambda g=g, cc=cc: LNchunk(g, cc))
                for _ in range(2 if i + 1 < N else KC):
                    lnq.pop(0)()
        assert not lnq

    WPW2 = reg(O_W, 16384).rearrange("p (c n) -> p c n", c=KC)
    T_WPW2 = Tile("wpw2", (O_W, 16384))
    B2T = reg(O_O, 4096, F32)
    T_B2 = Tile("b2t", (O_O, 4096))

    def pw2_ln(s):
        P.op("pool", lambda e: e.dma_start(out=WPW2, in_=pw2_d.rearrange("(c p) n -> p c n", p=128)), writes=[T_WPW2], dma=True)
        P.op("sp", lambda e: e.dma_start(out=B2T, in_=pw2b_d[0:1, :].partition_broadcast(128)), writes=[T_B2], dma=True)
        load_ln_params(2)
        ffn_prefetch(1)

        def mm_fn(tt, ba, bb):
            for half, bnk in ((0, ba), (1, bb)):
                for c in range(KC):
                    mm(psb(bnk), hT[:, c, tt * 128:(tt + 1) * 128], WPW2[:, c, half * 512:(half + 1) * 512], c == 0, False,
                       reads=[HT[tt], T_WPW2], writes=[PB[bnk]])
                mm(psb(bnk), onesA, B2T[:, half * 512:(half + 1) * 512], False, True, reads=[T_CONST, T_B2], writes=[PB[bnk]])

        def y_fn(tt, slot, ba, bb):
            yv, Ty = LN_Y[slot], T_LNY[slot]
            assert bb == ba + 1
            P.op("dve", lambda e: e.scalar_tensor_tensor(out=yv, in0=hres[:, tt, :], scalar=ALPHA, in1=pswide(ba), op0=ALU.mult, op1=ALU.add),
                 reads=[HRES[tt], PB[ba], PB[bb]], writes=[Ty])

        return ln_site(s, mm_fn, y_fn, ((0, 1), (2, 3), (4, 5)), (6, 7), want_T=True, out_dram=False, defer=True)

    def dump_f32(view_tiles):
        P.barrier()
        P.op("sp", lambda e: e.dma_start(out=dbg_d, in_=reg(O_H, NT * 4096, F32)), dma=True)

    def dump_bf(off):
        P.barrier()
        P.op("sp", lambda e: e.dma_start(out=dbgb_d, in_=reg(off, KC * TB)), dma=True)

    stages = ["attn", "ln1", "ffn0", "conv", "ln3", "all"]
    lim = stages.index(upto)
    USE_BARRIERS = True

    def pb():
        if USE_BARRIERS:
            P.barrier()

    for s in range(NSEQ):
        layer0(s)
        if lim == 0:
            dump_bf(O_O)
            break
        pre = wout_ln1(s)
        if lim == 1:
            for f in pre:
                f()
            dump_f32(None)
            dump_bf(O_HT)
            break
        ffn(s, 0, 1, last=False, pre=pre)
        if lim == 2:
            dump_f32(None)
            dump_bf(O_HT)
            break
        conv_mixer(s)
        if lim == 3:
            dump_bf(O_HT)
            break
        pre = pw2_ln(s)
        if lim == 4:
            for f in pre:
                f()
            dump_f32(None)
            dump_bf(O_HT)
            break
        ffn(s, 1, 3, last=True, prefetched=True, pre=pre)
    P.barrier()
    P.emit(nc)
    return nc, P


def host_consts(T):
    bf = ml_dtypes.bfloat16
    j = np.arange(128)[:, None]
    k = np.arange(128)[None, :]
    cb = np.zeros((128, 2048), np.float32)
    cb[:, 0:128] = (j == k)
    cb[:, 128:256] = 1.0
    cb[:, 256:384] = -(j >= k).astype(np.float32)
    cb[:, 384:512] = -1.0
    cb[:, 512:640] = np.where(j >= k, NEG, 0.0)
    cb[:, 1024:1152] = np.where(j > k, NEG, 0.0)
    cf = np.zeros((128, 512), np.float32)
    cf[:, 0:128] = 1.0 / 128
    cf[:, 128:256] = 1.0 / 1024
    cf[:, 256:384] = np.eye(128, dtype=np.float32) * np.float32(ALPHA)
    cf[:, 384:512] = np.eye(128, dtype=np.float32)
    t = np.arange(T)
    aug = np.zeros((4, 2, 4, T), np.float32)
    for h in range(4):
        sl = SLOPES[h]
        aug[h, 0, 0] = 1.0
        aug[h, 0, 1] = 1.0
        aug[h, 0, 2] = -sl * 128 * (t // 128)
        aug[h, 0, 3] = -sl * (t % 128)
        aug[h, 1, 0] = sl * 128 * (t // 128)
        aug[h, 1, 1] = sl * (t % 128)
        aug[h, 1, 2] = 1.0
        aug[h, 1, 3] = 1.0
    return cb.astype(bf), cf, aug.astype(bf)


def pack_inputs(inp, T):
    f = lambda a: np.ascontiguousarray(np.asarray(a, dtype=np.float32))
    cb, cf, aug = host_consts(T)
    vec = np.zeros((128, 560), np.float32)
    pw1b = f(inp["conv_pw1_b"])[0]
    vec[:, 0:8] = pw1b[:D].reshape(8, 128).T
    vec[:, 8:16] = pw1b[D:].reshape(8, 128).T
    vec[:, 16:24] = f(inp["conv_dw_b"])[0].reshape(8, 128).T
    vec[:, 24:32] = f(inp["conv_ln_g"])[0].reshape(8, 128).T
    vec[:, 32:40] = f(inp["conv_ln_b"])[0].reshape(8, 128).T
    dww = f(inp["conv_dw_w"])[0, :, 0, :]
    vec[:, 40:288] = dww.reshape(31, 8, 128).transpose(2, 1, 0).reshape(128, 248)
    vec[:, 288] = f(inp["diff_subln_g"])[0]
    vec[:, 289:353] = f(inp["diff_lambda_q1"])[0][None, :]
    vec[:, 353:417] = f(inp["diff_lambda_k1"])[0][None, :]
    vec[:, 417:481] = f(inp["diff_lambda_q2"])[0][None, :]
    vec[:, 481:545] = f(inp["diff_lambda_k2"])[0][None, :]
    ln_g = np.stack([f(inp["mix_ln_g"])[0], f(inp["ffn_ln_g"])[0], f(inp["mix_ln_g"])[1], f(inp["ffn_ln_g"])[1]])
    ln_b = np.stack([f(inp["mix_ln_b"])[0], f(inp["ffn_ln_b"])[0], f(inp["mix_ln_b"])[1], f(inp["ffn_ln_b"])[1]])
    shared = {
        "attn_w_in": f(inp["attn_w_in"])[0], "attn_w_out": f(inp["attn_w_out"])[0],
        "conv_pw1_w": f(inp["conv_pw1_w"])[0], "conv_pw2_w": f(inp["conv_pw2_w"])[0],
        "ffn_w_gate": f(inp["ffn_w_gate"]), "ffn_w_up": f(inp["ffn_w_up"]), "ffn_w_down": f(inp["ffn_w_down"]),
        "ln_g": np.ascontiguousarray(ln_g), "ln_b": np.ascontiguousarray(ln_b),
        "pw2_b": f(inp["conv_pw2_b"]).reshape(1, D),
        "cb": cb, "cf": cf, "vec": vec, "aug": aug,
    }
    return shared


_CACHE = {}


def kernel(**inputs):
    x = np.ascontiguousarray(np.asarray(inputs["x"], dtype=np.float32))
    B, T, _ = x.shape
    ncores = 8
    nseq = B // ncores
    key = (T, nseq)
    if key not in _CACHE:
        _CACHE[key] = build_program(T, nseq, "all")[0]
    nc = _CACHE[key]
    shared = pack_inputs(inputs, T)
    in_maps = []
    for c in range(ncores):
        m = dict(shared)
        m["x"] = np.ascontiguousarray(x[c * nseq:(c + 1) * nseq])
        in_maps.append(m)
    res = run_bass_kernel_spmd(nc, in_maps, core_ids=list(range(ncores)))
    out = np.concatenate([np.asarray(r["out"], dtype=np.float32) for r in res.results], axis=0)
    return out
```

```python
import math
import numpy as np
import ml_dtypes
import concourse.bass as bass
import concourse.mybir as mybir
from concourse.bass_utils import run_bass_kernel_spmd

F32 = mybir.dt.float32
BF16 = mybir.dt.bfloat16
AF = mybir.ActivationFunctionType
ALU = mybir.AluOpType

D = 1024
KC = 8
FF = 2816
NFC = 22
INW = 3072
ALPHA = float(4.0 ** 0.25)
EPS = 1e-5
SLOPES = [2.0 ** (-8.0 * (h + 1) / 4) for h in range(4)]
LAMBDA_INIT = 0.8 - 0.6 * math.exp(0.0)
NEG = -30000.0
NDS_SP = 16
NDS_POOL = 8
NDS = NDS_SP + NDS_POOL
ENGS = ("pe", "act", "dve", "pool", "sp")


class Tile:
    __slots__ = ("name", "writers", "readers", "ranges", "alias")
    REG = []

    def __init__(self, name, *ranges):
        self.name = name
        self.writers = []
        self.readers = []
        self.ranges = [(o, o + n) for (o, n) in ranges]
        self.alias = []
        if self.ranges:
            lo = min(a for a, _ in self.ranges)
            hi = max(b for _, b in self.ranges)
            for u in Tile.REG:
                ulo, uhi = u.ranges[0][0], u.ranges[-1][1]
                if ulo >= hi or uhi <= lo:
                    continue
                if any(a < d and c < b for (a, b) in self.ranges for (c, d) in u.ranges):
                    self.alias.append(u)
                    u.alias.append(self)
            self.ranges.sort()
            Tile.REG.append(self)


class Op:
    __slots__ = ("eng", "fn", "idx", "waits", "signal", "dma", "dma_sem", "dma_val", "clock", "rank")


class Prog:
    def __init__(self):
        self.eng_ops = {e: [] for e in ENGS}
        self.seen = {e: {} for e in ENGS}
        self.dma_count = [0] * NDS
        self.dma_rr = {"sp": 0, "pool": NDS_SP}
        self.dma_clock = {}
        self.nops = 0

    def op(self, eng, fn, reads=(), writes=(), dma=False, extra=()):
        o = Op()
        o.eng, o.fn, o.dma, o.signal = eng, fn, dma, False
        deps = list(extra)
        for t in reads:
            deps += t.writers
            for u in t.alias:
                deps += u.writers
        for t in writes:
            deps += t.writers
            deps += t.readers
            for u in t.alias:
                deps += u.writers
                deps += u.readers
        if dma:
            base, cnt = (0, NDS_SP) if eng == "sp" else (NDS_SP, NDS_POOL)
            sem = self.dma_rr[eng]
            self.dma_rr[eng] = base + (sem - base + 1) % cnt
            prev = self.dma_count[sem]
            if prev > 0:
                deps.append(("D", sem, prev))
            self.dma_count[sem] = prev + 16
            o.dma_sem, o.dma_val = sem, prev + 16
        seen = self.seen[eng]
        waits = {}
        for ev in deps:
            if ev[0] == "E":
                _, e2, i2 = ev
                if e2 == eng and eng == "pe":
                    continue
                if seen.get(("E", e2), -1) >= i2:
                    continue
                k = ("E", e2)
                if waits.get(k, -1) < i2:
                    waits[k] = i2
            else:
                _, sem, val = ev
                if seen.get(("D", sem), 0) >= val:
                    continue
                k = ("D", sem)
                if waits.get(k, 0) < val:
                    waits[k] = val
        for k, val in waits.items():
            if k[0] == "E":
                prod = self.eng_ops[k[1]][val]
                prod.signal = True
                clock = prod.clock
            else:
                clock = self.dma_clock[(k[1], val)]
            for kk, vv in clock.items():
                if seen.get(kk, -1) < vv:
                    seen[kk] = vv
            if seen.get(k, -1) < val:
                seen[k] = val
        o.waits = list(waits.items())
        o.idx = len(self.eng_ops[eng])
        o.clock = dict(seen)
        self.eng_ops[eng].append(o)
        self.nops += 1
        if dma:
            ev = ("D", o.dma_sem, o.dma_val)
            self.dma_clock[(o.dma_sem, o.dma_val)] = o.clock
        else:
            ev = ("E", eng, o.idx)
        for t in writes:
            t.writers = [ev]
            t.readers = []
        for t in reads:
            if ev[0] == "E":
                t.readers = [r for r in t.readers if not (r[0] == "E" and r[1] == eng)]
            t.readers.append(ev)
        return o

    def barrier(self):
        deps = []
        for e in ENGS:
            if e != "sp":
                for o in reversed(self.eng_ops[e]):
                    if not o.dma and o.fn is not None:
                        deps.append(("E", e, o.idx))
                        break
        for sem in range(NDS):
            if self.dma_count[sem] > 0:
                deps.append(("D", sem, self.dma_count[sem]))
        b = self.op("sp", lambda e: e.nop(), extra=deps)
        ev = ("E", "sp", b.idx)
        for e in ENGS:
            if e != "sp":
                self.op(e, None, extra=[ev])

    def emit(self, nc):
        for e in ENGS:
            r = 0
            for o in self.eng_ops[e]:
                if o.signal:
                    r += 1
                o.rank = r
        from contextlib import ExitStack
        with ExitStack() as st:
            esem = {e: st.enter_context(nc.semaphore("s_" + e)) for e in ENGS}
            dsem = [st.enter_context(nc.semaphore("d_%d" % i)) for i in range(NDS)]
            block = st.enter_context(nc.Block())

            def run(engname, eng):
                for o in self.eng_ops[engname]:
                    for k, val in o.waits:
                        if k[0] == "E":
                            eng.wait_ge(esem[k[1]], self.eng_ops[k[1]][val].rank)
                        else:
                            eng.wait_ge(dsem[k[1]], val)
                    if o.fn is None:
                        continue
                    ins = o.fn(eng)
                    if o.dma:
                        ins.then_inc(dsem[o.dma_sem], 16)
                    elif o.signal:
                        ins.then_inc(esem[engname], 1)

            @block.tensor
            def _(e):
                run("pe", e)

            @block.scalar
            def _(e):
                run("act", e)

            @block.vector
            def _(e):
                run("dve", e)

            @block.gpsimd
            def _(e):
                run("pool", e)

            @block.sync
            def _(e):
                run("sp", e)


def build_program(T, NSEQ, upto="all"):
    NT = T // 128
    NG = T // 512
    TB = T * 2
    nc = bass.Bass("TRN2", target_bir_lowering=False)
    P = Prog()
    Tile.REG = []

    def din(name, shape, dt=F32):
        return nc.dram_tensor(name, list(shape), dt, kind="ExternalInput").ap()

    x_d = din("x", [NSEQ, T, D])
    w_in_d = din("attn_w_in", [D, INW])
    w_out_d = din("attn_w_out", [D, D])
    pw1_d = din("conv_pw1_w", [D, 2 * D])
    pw2_d = din("conv_pw2_w", [D, D])
    wg_d = din("ffn_w_gate", [2, D, FF])
    wu_d = din("ffn_w_up", [2, D, FF])
    wd_d = din("ffn_w_down", [2, FF, D])
    lng_d = din("ln_g", [4, D])
    lnb_d = din("ln_b", [4, D])
    pw2b_d = din("pw2_b", [1, D])
    cb_d = din("cb", [128, 2048], BF16)
    cf_d = din("cf", [128, 512])
    vec_d = din("vec", [128, 560])
    aug_d = din("aug", [4, 2, 4, T], BF16)
    out_d = nc.dram_tensor("out", [NSEQ, T, D], F32, kind="ExternalOutput").ap()
    if upto != "all":
        dbg_d = nc.dram_tensor("dbg", [128, NT * 1024], F32, kind="ExternalOutput").ap()
        dbgb_d = nc.dram_tensor("dbgb", [128, KC * T], BF16, kind="ExternalOutput").ap()

    ARENA_KB = 200
    arena = nc.alloc_sbuf_tensor("arena", [128, ARENA_KB * 512], BF16)

    def reg(off, nbytes, dt=BF16):
        assert off % 4 == 0 and nbytes % 4 == 0 and off + nbytes <= ARENA_KB * 1024, (off, nbytes)
        a = arena[:, off // 2:(off + nbytes) // 2]
        if dt == F32:
            a = a.bitcast(F32)
        return a

    PSALL = nc.alloc_psum_tensor("psall", [128, 4096], F32)
    PB = [Tile("psb%d" % i) for i in range(8)]

    def psb(i):
        return PSALL[:, i * 512:(i + 1) * 512]

    def psbf(i):
        return PSALL[:, i * 512:(i + 1) * 512].bitcast(BF16)

    def pswide(i):
        return PSALL[:, i * 512:(i + 2) * 512]

    def pspair(i):
        return PSALL[:, i * 512:(i + 2) * 512].rearrange("p (h n) -> p h n", h=2)

    O_CONST = 0
    CB = reg(0, 4096)
    CF = reg(4096, 2048, F32)
    VEC = reg(6144, 2240, F32)
    SM = reg(8384, 1024, F32)
    O_H = 9728
    SZ_H = 64 * 1024
    O_HT = O_H + SZ_H
    SZ_HT = 32 * 1024
    O_O = O_HT + SZ_HT
    SZ_O = 32 * 1024
    O_W = O_O + SZ_O
    SZ_W = 24 * 1024
    O_M = O_W + SZ_W
    SZ_M = 38 * 1024
    assert O_M + SZ_M <= ARENA_KB * 1024

    ident = CB[:, 0:128]
    ones_b = CB[:, 128:256]
    negtinc = CB[:, 256:384]
    negones = CB[:, 384:512]
    masksb = CB[:, 512:1024]
    maskdf = CB[:, 1024:1536]
    zeros_b = CB[:, 1536:2048]
    onesA = CF[:, 0:128]
    onesB = CF[:, 128:256]
    alphaI = CF[:, 256:384]
    identF = CF[:, 384:512]
    V_PW1BV, V_PW1BG, V_DWB, V_CLNG, V_CLNB, V_DWW, V_SUBG, V_LAM = 0, 8, 16, 24, 32, 40, 288, 289
    T_CONST = Tile("const")

    hres = reg(O_H, NT * 4096, F32).rearrange("p (t d) -> p t d", t=NT)
    hT = reg(O_HT, KC * TB).rearrange("p (c t) -> p c t", c=KC)
    HRES = [Tile("hres%d" % i, (O_H + i * 4096, 4096)) for i in range(NT)]
    HT = [Tile("hT%d" % i, *[(O_HT + c * TB + i * 256, 256) for c in range(KC)]) for i in range(NT)]

    P.op("sp", lambda e: e.dma_start(out=CB, in_=cb_d), writes=[T_CONST], dma=True)
    P.op("sp", lambda e: e.dma_start(out=CF, in_=cf_d), writes=[T_CONST], dma=True)
    P.op("sp", lambda e: e.dma_start(out=VEC[:, 0:560], in_=vec_d), writes=[T_CONST], dma=True)

    T_LAM = Tile("lam")
    lamtmp = SM[:, 64:128]

    def lam_ops():
        q1 = VEC[:, V_LAM:V_LAM + 64]
        k1 = VEC[:, V_LAM + 64:V_LAM + 128]
        q2 = VEC[:, V_LAM + 128:V_LAM + 192]
        k2 = VEC[:, V_LAM + 192:V_LAM + 256]
        P.op("dve", lambda e: e.tensor_tensor(out=lamtmp, in0=q1, in1=k1, op=ALU.mult), reads=[T_CONST], writes=[T_LAM])
        P.op("dve", lambda e: e.reduce_sum(out=SM[:, 0:1], in_=lamtmp, axis=mybir.AxisListType.X), reads=[T_LAM], writes=[T_LAM])
        P.op("dve", lambda e: e.tensor_tensor(out=lamtmp, in0=q2, in1=k2, op=ALU.mult), reads=[T_CONST, T_LAM], writes=[T_LAM])
        P.op("dve", lambda e: e.reduce_sum(out=SM[:, 1:2], in_=lamtmp, axis=mybir.AxisListType.X), reads=[T_LAM], writes=[T_LAM])
        P.op("act", lambda e: e.activation(out=SM[:, 2:4], in_=SM[:, 0:2], func=AF.Exp), reads=[T_LAM], writes=[T_LAM])
        P.op("dve", lambda e: e.scalar_tensor_tensor(out=SM[:, 4:5], in0=SM[:, 3:4], scalar=-LAMBDA_INIT, in1=SM[:, 2:3],
                                                     op0=ALU.add, op1=ALU.subtract), reads=[T_LAM], writes=[T_LAM])
        P.op("dve", lambda e: e.tensor_scalar(out=SM[:, 5:6], in0=VEC[:, V_SUBG:V_SUBG + 1], scalar1=(1.0 - LAMBDA_INIT),
                                              scalar2=None, op0=ALU.mult), reads=[T_CONST, T_LAM], writes=[T_LAM])

    lam_ops()
    neglam = SM[:, 4:5]
    gsub = SM[:, 5:6]

    evac_rr = [0]

    def evac_eng():
        evac_rr[0] ^= 1
        return "dve" if evac_rr[0] else "act"

    def copy_op(eng, out, in_, reads, writes, scale=None):
        if eng == "act":
            if scale is None:
                P.op("act", lambda e: e.activation(out=out, in_=in_, func=AF.Copy), reads=reads, writes=writes)
            else:
                P.op("act", lambda e: e.activation(out=out, in_=in_, func=AF.Copy, scale=scale), reads=reads, writes=writes)
        else:
            if scale is None:
                P.op(eng, lambda e: e.tensor_copy(out=out, in_=in_), reads=reads, writes=writes)
            else:
                P.op(eng, lambda e: e.tensor_scalar(out=out, in0=in_, scalar1=scale, scalar2=None, op0=ALU.mult),
                     reads=reads, writes=writes)

    def mm(out, lhsT, rhs, start, stop, reads, writes, skip=False):
        if skip:
            P.op("pe", lambda e: e.matmul(out, lhsT=lhsT, rhs=rhs, start=start, stop=stop, skip_group_check=True),
                 reads=reads, writes=writes)
        else:
            P.op("pe", lambda e: e.matmul(out, lhsT=lhsT, rhs=rhs, start=start, stop=stop), reads=reads, writes=writes)

    LN_Y = [reg(O_M + i * 4096, 4096, F32) for i in range(3)]
    T_LNY = [Tile("lny%d" % i, (O_M + i * 4096, 4096)) for i in range(3)]
    LN_HB = [reg(O_M + 12288 + i * 2048, 2048) for i in range(3)]
    T_LNHB = [Tile("lnhb%d" % i, (O_M + 12288 + i * 2048, 2048)) for i in range(3)]
    LN_G = reg(O_M + 18432, 4096, F32)
    LN_B = reg(O_M + 22528, 4096, F32)
    T_LNGB = Tile("lngb", (O_M + 18432, 8192))
    T_ST = [Tile("lnst%d" % i) for i in range(3)]

    def load_ln_params(idx):
        P.op("sp", lambda e: e.dma_start(out=LN_G, in_=lng_d[idx:idx + 1, :].partition_broadcast(128)), writes=[T_LNGB], dma=True)
        P.op("sp", lambda e: e.dma_start(out=LN_B, in_=lnb_d[idx:idx + 1, :].partition_broadcast(128)), writes=[T_LNGB], dma=True)

    def ln_slots(slot):
        o = 128 + slot * 32
        return (SM[:, o:o + 12], SM[:, o + 12:o + 14], SM[:, o + 14:o + 15], SM[:, o + 15:o + 16], SM[:, o + 16:o + 17])

    def ln_A(tt, slot):
        st, mv, lnv, rstd, nmr = ln_slots(slot)
        Tst = T_ST[slot]
        yv, Ty = LN_Y[slot], T_LNY[slot]
        P.op("dve", lambda e: e.bn_stats(out=st[:, 0:6], in_=yv[:, 0:512]), reads=[Ty], writes=[Tst])
        P.op("dve", lambda e: e.bn_stats(out=st[:, 6:12], in_=yv[:, 512:1024]), reads=[Ty, Tst], writes=[Tst])
        P.op("dve", lambda e: e.bn_aggr(out=mv, in_=st), reads=[Tst], writes=[Tst])
        P.op("dve", lambda e: e.tensor_scalar(out=lnv, in0=mv[:, 0:1], scalar1=-1.0, scalar2=None, op0=ALU.mult), reads=[Tst], writes=[Tst])
        P.op("act", lambda e: e.activation(out=rstd, in_=mv[:, 1:2], func=AF.Ln, bias=EPS_AP), reads=[Tst, T_CONST], writes=[Tst])
        P.op("act", lambda e: e.activation(out=rstd, in_=rstd, func=AF.Exp, scale=-0.5), reads=[Tst], writes=[Tst])
        P.op("act", lambda e: e.activation(out=nmr, in_=lnv, func=AF.Identity, scale=rstd), reads=[Tst], writes=[Tst])

    def ln_B(tt, slot):
        st, mv, lnv, rstd, nmr = ln_slots(slot)
        Tst = T_ST[slot]
        yv, Ty = LN_Y[slot], T_LNY[slot]
        P.op("act", lambda e: e.activation(out=yv, in_=yv, func=AF.Identity, bias=nmr, scale=rstd), reads=[Ty, Tst], writes=[Ty])

    def ln_C(tt, slot):
        yv, Ty = LN_Y[slot], T_LNY[slot]
        P.op("pool", lambda e: e.tensor_tensor(out=yv, in0=yv, in1=LN_G, op=ALU.mult), reads=[Ty, T_LNGB], writes=[Ty])

    def ln_D(s, tt, slot, want_T, out_dram):
        yv, Ty = LN_Y[slot], T_LNY[slot]
        P.op("dve", lambda e: e.tensor_tensor(out=hres[:, tt, :], in0=yv, in1=LN_B, op=ALU.add), reads=[Ty, T_LNGB], writes=[HRES[tt]])
        if out_dram:
            P.op("sp", lambda e: e.dma_start(out=out_d[s, tt * 128:(tt + 1) * 128, :], in_=hres[:, tt, :]), reads=[HRES[tt]], dma=True)
        if want_T:
            hb = LN_HB[slot]
            Thb = T_LNHB[slot]
            P.op("act", lambda e: e.activation(out=hb, in_=hres[:, tt, :], func=AF.Copy), reads=[HRES[tt]], writes=[Thb])

    def ln_E(tt, slot, tbank):
        hb = LN_HB[slot]
        Thb = T_LNHB[slot]
        for c in range(KC):
            P.op("pe", lambda e, c=c: e.transpose(psbf(tbank)[:, c * 128:(c + 1) * 128], hb[:, c * 128:(c + 1) * 128], ident),
                 reads=[Thb, T_CONST], writes=[PB[tbank]])
        P.op("act", lambda e: e.activation(out=hT[:, :, tt * 128:(tt + 1) * 128],
                                           in_=psbf(tbank)[:, 0:1024].rearrange("p (c t) -> p c t", c=KC), func=AF.Copy),
             reads=[PB[tbank]], writes=[HT[tt]])

    def ln_site(s, mm_fn, y_fn, pairs, tbanks, want_T, out_dram, defer=False):
        def stage(step):
            if step < NT:
                ba, bb = pairs[step % 3]
                mm_fn(step, ba, bb)
            t = step - 1
            if 0 <= t < NT:
                ba, bb = pairs[t % 3]
                y_fn(t, t % 3, ba, bb)
                ln_A(t, t % 3)
            t = step - 2
            if 0 <= t < NT:
                ln_B(t, t % 3)
            t = step - 3
            if 0 <= t < NT:
                ln_C(t, t % 3)
                ln_D(s, t, t % 3, want_T, out_dram)
            t = step - 4
            if want_T and 0 <= t < NT:
                ln_E(t, t % 3, tbanks[t % 2])

        for step in range(NT):
            stage(step)
        drain = [(lambda st=st: stage(st)) for st in range(NT, NT + 4)]
        if defer:
            return drain
        for f in drain:
            f()
        return []

    EPS_AP = SM[:, 6:7]
    P.op("pool", lambda e: e.memset(SM[:, 6:7], EPS), writes=[T_CONST])

    xT = reg(O_H, KC * TB).rearrange("p (c t) -> p c t", c=KC)
    XT = [Tile("xT%d" % i, *[(O_H + c * TB + i * 256, 256) for c in range(KC)]) for i in range(NT)]
    Vdf = reg(O_H + KC * TB, NT * 1024).rearrange("p (t c) -> p t c", t=NT)
    VDF = [Tile("vdf%d" % i, (O_H + KC * TB + i * 1024, 1024)) for i in range(NT)]
    O_SETS = O_H + KC * TB + NT * 1024
    SET_SZ = 4 * TB + NT * 512

    class USet:
        pass

    sets = []
    for si in range(2):
        u = USet()
        base = O_SETS + si * SET_SZ
        u.T = [reg(base + j * TB, TB) for j in range(4)]
        u.TT = [[Tile("s%dT%dg%d" % (si, j, g), (base + j * TB + g * 1024, 1024)) for g in range(NG)] for j in range(4)]
        u.Vp = reg(base + 4 * TB, NT * 512).rearrange("p (t c) -> p t c", t=NT)
        u.VP = [Tile("s%dvp%d" % (si, i), (base + 4 * TB + i * 512, 512)) for i in range(NT)]
        u.W = reg(O_W + si * 8192, 8192).rearrange("p (c f) -> p c f", c=KC)
        u.TW = Tile("s%dW" % si, (O_W + si * 8192, 8192))
        sets.append(u)
    assert O_SETS + 2 * SET_SZ <= O_O
    oT = reg(O_O, KC * TB).rearrange("p (c t) -> p c t", c=KC)
    OT = [[Tile("oT%dg%d" % (c, g), (O_O + c * TB + g * 1024, 1024)) for g in range(NG)] for c in range(KC)]
    WV = reg(O_W + 16384, 8192).rearrange("p (c f) -> p c f", c=KC)
    T_WV = Tile("wv", (O_W + 16384, 8192))
    XB = [reg(O_M + 0, 2048), reg(O_M + 2048, 2048), reg(O_M + 29696, 2048), reg(O_M + 31744, 2048)]
    T_XB = [Tile("xb%d" % i, (O_M + (0, 2048, 29696, 31744)[i], 2048)) for i in range(4)]
    EB2 = reg(O_M + 4096, 4096, F32).rearrange("p (h n) -> p h n", h=2)
    T_EB2 = Tile("eb2", (O_M + 4096, 4096))
    SPB2 = [reg(O_M + 8192 + i * 2048, 2048).rearrange("p (h n) -> p h n", h=2) for i in range(2)]
    T_SPB2 = [Tile("spb2_%d" % i, (O_M + 8192 + i * 2048, 2048)) for i in range(2)]
    _sps_off = (O_M + 12288, O_M + 33792)
    SPS2 = [reg(o, 2048).rearrange("p (h n) -> p h n", h=2) for o in _sps_off]
    T_SPS2 = [Tile("sps2_%d" % i, (_sps_off[i], 2048)) for i in range(2)]
    _ab_off = (O_M + 35840, O_M + 25600)
    AB2 = [reg(o, 2048).rearrange("p (h n) -> p h n", h=2) for o in _ab_off]
    T_AB2 = [Tile("ab2_%d" % i, (_ab_off[i], 2048)) for i in range(2)]
    EDF = [reg(O_M + 14336 + i * 1024, 1024) for i in range(3)]
    T_EDF = [Tile("edf%d" % i, (O_M + 14336 + i * 1024, 1024)) for i in range(3)]
    PP = [reg(O_M + 17408 + i * 2048, 2048, F32) for i in range(6)]
    T01 = reg(O_M + 17408 + 2 * 2048, 4096, F32)
    T_PP = [Tile("pp%d" % i, (O_M + 17408 + i * 2048, 2048)) for i in range(6)]
    assert 17408 + 6 * 2048 <= SZ_M

    w_in_v = w_in_d.rearrange("(c p) f -> p c f", p=128)

    def load_unit_weights(uidx, S):
        W, TW = S.W, S.TW
        if uidx < 4:
            qo, ko, vo = uidx * 128, 512 + uidx * 128, 1024 + uidx * 128
            P.op("pool", lambda e: e.dma_start(out=W[:, :, 0:128], in_=w_in_v[:, :, qo:qo + 128]), writes=[TW], dma=True)
            P.op("pool", lambda e: e.dma_start(out=W[:, :, 128:256], in_=w_in_v[:, :, ko:ko + 128]), writes=[TW], dma=True)
            P.op("pool", lambda e: e.dma_start(out=W[:, :, 256:320], in_=w_in_v[:, :, vo:vo + 64]), writes=[TW], dma=True)
            P.op("pool", lambda e: e.dma_start(out=W[:, :, 448:512], in_=w_in_v[:, :, vo + 64:vo + 128]), writes=[TW], dma=True)
        else:
            h = uidx - 4
            qo, ko = 1536 + h * 128, 2048 + h * 128
            P.op("pool", lambda e: e.dma_start(out=W[:, :, 0:128], in_=w_in_v[:, :, qo:qo + 128]), writes=[TW], dma=True)
            P.op("pool", lambda e: e.dma_start(out=W[:, :, 128:256], in_=w_in_v[:, :, ko:ko + 128]), writes=[TW], dma=True)

    def proj_thunks(uidx, S, pbanks):
        th = []
        W, TW = S.W, S.TW
        bank_rr = [0]

        def nextbank():
            b = pbanks[bank_rr[0] % len(pbanks)]
            bank_rr[0] += 1
            return b

        def prep():
            if uidx < 4:
                if uidx < 2:
                    P.op("pool", lambda e: e.memset(S.T[1][64:128, :], 0.0), writes=S.TT[1])
                    P.op("pool", lambda e: e.memset(S.T[2][0:64, :], 0.0), writes=S.TT[2])
            else:
                h = uidx - 4
                if uidx < 6:
                    P.op("pool", lambda e: e.memset(S.T[0][64:128, :], 0.0), writes=S.TT[0])
                    P.op("pool", lambda e: e.memset(S.T[3][0:64, :], 0.0), writes=S.TT[3])
                P.op("sp", lambda e: e.dma_start(out=S.T[1][64:68, :], in_=aug_d[h, 0]), writes=S.TT[1], dma=True)
                P.op("sp", lambda e: e.dma_start(out=S.T[2][0:4, :], in_=aug_d[h, 0]), writes=S.TT[2], dma=True)
                P.op("sp", lambda e: e.dma_start(out=S.T[0][64:68, :], in_=aug_d[h, 1]), writes=S.TT[0], dma=True)
                P.op("sp", lambda e: e.dma_start(out=S.T[3][0:4, :], in_=aug_d[h, 1]), writes=S.TT[3], dma=True)
        th.append(prep)

        def qk(g, which):
            def f():
                b = nextbank()
                co = 0 if which == "q" else 128
                for c in range(KC):
                    mm(psb(b), W[:, c, co:co + 128], xT[:, c, g * 512:(g + 1) * 512], c == 0, c == KC - 1,
                       reads=[TW] + XT[4 * g:4 * g + 4], writes=[PB[b]])
                gs = slice(g * 512, (g + 1) * 512)
                if which == "q":
                    copy_op("dve", S.T[1][0:64, gs], psb(b)[0:64, :], [PB[b]], [S.TT[1][g]], scale=0.125)
                    copy_op("dve", S.T[2][64:128, gs], psb(b)[64:128, :], [PB[b]], [S.TT[2][g]], scale=0.125)
                else:
                    if uidx < 4:
                        copy_op("dve", S.T[0][:, gs], psb(b), [PB[b]], [S.TT[0][g]])
                    else:
                        copy_op("dve", S.T[0][0:64, gs], psb(b)[0:64, :], [PB[b]], [S.TT[0][g]])
                        copy_op("dve", S.T[3][64:128, gs], psb(b)[64:128, :], [PB[b]], [S.TT[3][g]])
            return f

        def vproj(tt):
            def f():
                b = nextbank()
                for c in range(KC):
                    mm(psb(b)[:, 0:256], xT[:, c, tt * 128:(tt + 1) * 128], W[:, c, 256:512], c == 0, c == KC - 1,
                       reads=[TW, XT[tt]], writes=[PB[b]])
                copy_op("dve", S.Vp[:, tt, :], psb(b)[:, 0:256], [PB[b]], [S.VP[tt]])
            return f

        for g in range(NG):
            th.append(qk(g, "k"))
        for g in range(NG):
            th.append(qk(g, "q"))
        if uidx < 4:
            for tt in range(NT):
                th.append(vproj(tt))
        return th

    def sb_items(uidx, S):
        items = []
        for g in range(NG):
            kmax = min(4 * g + 3, NT - 1)
            for kb in range(kmax, -1, -1):
                items.append((g, kb, kb == kmax, kb == 0))
        n = len(items)
        ZP, DP = 0, 2
        OB = (4, 5)
        cur_of = []
        cnt = 0
        for (g, kb, first, last) in items:
            if first:
                cnt = 0
            cur_of.append(cnt % 2)
            if kb > 0:
                cnt += 1
        Qs = (S.T[1], S.T[2])
        QTs = (S.TT[1], S.TT[2])

        def geom(i):
            g, kb, first, last = items[i]
            c0 = max(0, kb - 4 * g) * 128
            diag = kb >= 4 * g
            kt = S.T[0][:, kb * 128:(kb + 1) * 128]
            return g, kb, first, last, c0, diag, kt

        def Zst(i):
            g, kb, first, last, c0, diag, kt = geom(i)
            for h in range(2):
                bnk = ZP + h
                mm(psb(bnk)[:, c0:512], kt, Qs[h][:, g * 512 + c0:(g + 1) * 512], True, True,
                   reads=[S.TT[0][kb // 4], QTs[h][g]], writes=[PB[bnk]])
                if diag:
                    mm(psb(bnk)[:, c0:c0 + 128], ident, masksb[:, 0:128], False, True, reads=[T_CONST], writes=[PB[bnk]], skip=True)

        def ESst(i):
            g, kb, first, last, c0, diag, kt = geom(i)
            spb, Tsp = SPB2[i % 2], T_SPB2[i % 2]
            P.op("act", lambda e: e.activation(out=EB2[:, :, c0:512], in_=pspair(ZP)[:, :, c0:512], func=AF.Exp),
                 reads=[PB[ZP], PB[ZP + 1]], writes=[T_EB2])
            P.op("act", lambda e: e.activation(out=spb[:, :, c0:512], in_=EB2[:, :, c0:512], func=AF.Ln, bias=1.0),
                 reads=[T_EB2], writes=[Tsp])

        def DDst(i):
            g, kb, first, last, c0, diag, kt = geom(i)
            spb, Tsp = SPB2[i % 2], T_SPB2[i % 2]
            cur = cur_of[i]
            if first:
                P.op("pool", lambda e: e.memset(SPS2[0], 0.0), writes=[T_SPS2[0]])
                P.op("pool", lambda e: e.memset(SPS2[1], 0.0), writes=[T_SPS2[1]])
            for h in range(2):
                bnk = DP + h
                mm(psb(bnk)[:, c0:512], kt, Qs[h][:, g * 512 + c0:(g + 1) * 512], True, False,
                   reads=[S.TT[0][kb // 4], QTs[h][g]], writes=[PB[bnk]])
                mm(psb(bnk)[:, c0:512], negtinc, spb[:, h, c0:512], False, first, reads=[T_CONST, Tsp], writes=[PB[bnk]])
                if not first:
                    mm(psb(bnk)[:, c0:512], negones, SPS2[cur][:, h, c0:512], False, True, reads=[T_CONST, T_SPS2[cur]], writes=[PB[bnk]])
                if diag:
                    mm(psb(bnk)[:, c0:c0 + 128], ident, masksb[:, 0:128], False, True, reads=[T_CONST], writes=[PB[bnk]], skip=True)

        def AAst(i):
            g, kb, first, last, c0, diag, kt = geom(i)
            ab, Ta = AB2[i % 2], T_AB2[i % 2]
            P.op("act", lambda e: e.activation(out=ab[:, :, c0:512], in_=pspair(DP)[:, :, c0:512], func=AF.Exp),
                 reads=[PB[DP], PB[DP + 1]], writes=[Ta])

        def SUst(i):
            g, kb, first, last, c0, diag, kt = geom(i)
            if kb > 0:
                spb, Tsp = SPB2[i % 2], T_SPB2[i % 2]
                cur = cur_of[i]
                nxt = 1 - cur
                P.op("dve", lambda e: e.tensor_tensor(out=SPS2[nxt][:, :, c0:512], in0=SPS2[cur][:, :, c0:512], in1=spb[:, :, c0:512], op=ALU.add),
                     reads=[T_SPS2[cur], Tsp], writes=[T_SPS2[nxt]])

        def AVst(i):
            g, kb, first, last, c0, diag, kt = geom(i)
            ob = OB[g % 2]
            ab, Ta = AB2[i % 2], T_AB2[i % 2]
            for h in range(2):
                vp = S.Vp[:, kb, h * 128:(h + 1) * 128]
                mm(psb(ob)[:, c0:512], vp, ab[:, h, c0:512], first and h == 0, last and h == 1, reads=[S.VP[kb], Ta], writes=[PB[ob]], skip=True)
            if last:
                copy_op("dve", oT[:, uidx, g * 512:(g + 1) * 512], psb(ob), [PB[ob]], [OT[uidx][g]])

        th = []

        def pro():
            Zst(0)
            ESst(0)
            if n > 1:
                Zst(1)
        th.append(pro)
        for r in range(n):
            def f(r=r):
                DDst(r)
                if r >= 1:
                    AVst(r - 1)
                if r + 1 < n:
                    ESst(r + 1)
                if r + 2 < n:
                    Zst(r + 2)
                AAst(r)
                SUst(r)
            th.append(f)
        th.append(lambda: AVst(n - 1))
        return th

    ONE_AP = SM[:, 7:8]
    P.op("pool", lambda e: e.memset(SM[:, 7:8], 1.0), writes=[T_CONST])

    DF_PENDING = []

    def df_items(uidx, S):
        h = uidx - 4
        c8 = 4 + h
        items = []
        for g in range(NG):
            kmax = min(4 * g + 3, NT - 1)
            for m in range(2):
                for kb in range(kmax, -1, -1):
                    items.append((g, m, kb, m == 0 and kb == kmax, m == 1 and kb == 0))
        SBK = (0, 1, 6)
        OBK = (2, 3)
        SMK = (4, 5)
        MSB = 7

        def stageA(i):
            g, m, kb, gfirst, glast = items[i]
            sbk = SBK[i % 3]
            c0 = max(0, kb - 4 * g) * 128
            diag = kb >= 4 * g
            K, KTl = (S.T[0], S.TT[0]) if m == 0 else (S.T[3], S.TT[3])
            Q, QTl = (S.T[1], S.TT[1]) if m == 0 else (S.T[2], S.TT[2])
            mm(psb(sbk)[:, c0:512], K[:, kb * 128:(kb + 1) * 128], Q[:, g * 512 + c0:(g + 1) * 512], True, True,
               reads=[KTl[kb // 4], QTl[g]], writes=[PB[sbk]])
            if diag:
                mm(psb(sbk)[:, c0:c0 + 128], ident, maskdf[:, 0:128], False, True, reads=[T_CONST], writes=[PB[sbk]], skip=True)

        def stageE(i):
            g, m, kb, gfirst, glast = items[i]
            sbk = SBK[i % 3]
            c0 = max(0, kb - 4 * g) * 128
            eb, Te = EDF[i % 3], T_EDF[i % 3]
            P.op("act", lambda e: e.activation(out=eb[:, c0:512], in_=psb(sbk)[:, c0:512], func=AF.Exp), reads=[PB[sbk]], writes=[Te])

        def stageB(i):
            g, m, kb, gfirst, glast = items[i]
            c0 = max(0, kb - 4 * g) * 128
            eb, Te = EDF[i % 3], T_EDF[i % 3]
            kfirst = kb == min(4 * g + 3, NT - 1)
            ob, sb_ = OBK[m], SMK[m]
            mm(psb(ob)[:, c0:512], Vdf[:, kb, h * 128:(h + 1) * 128], eb[:, c0:512], kfirst, kb == 0, reads=[VDF[kb], Te], writes=[PB[ob]], skip=True)
            mm(psb(sb_)[:, c0:512], ones_b, eb[:, c0:512], kfirst, kb == 0, reads=[T_CONST, Te], writes=[PB[sb_]], skip=True)
            if glast:
                post(g)

        pending = DF_PENDING

        def post(g):
            r0, r1, t0, t1, o, sq = PP
            Tr0, Tr1, Tt0, Tt1, To, Tsq = T_PP
            P.op("act", lambda e: e.activation(out=r0, in_=psb(SMK[0]), func=AF.Ln), reads=[PB[SMK[0]]], writes=[Tr0])
            P.op("dve", lambda e: e.tensor_copy(out=T01, in_=pswide(OBK[0])), reads=[PB[OBK[0]], PB[OBK[1]]], writes=[Tt0, Tt1])
            P.op("act", lambda e: e.activation(out=r1, in_=psb(SMK[1]), func=AF.Ln), reads=[PB[SMK[1]]], writes=[Tr1])

            def st1():
                P.op("act", lambda e: e.activation(out=r0, in_=r0, func=AF.Exp, scale=-1.0), reads=[Tr0], writes=[Tr0])
                P.op("act", lambda e: e.activation(out=r1, in_=r1, func=AF.Exp, scale=-1.0), reads=[Tr1], writes=[Tr1])
                pending.append([1, st2])

            def st2():
                P.op("dve", lambda e: e.tensor_tensor(out=t0, in0=t0, in1=r0, op=ALU.mult), reads=[Tt0, Tr0], writes=[Tt0])
                P.op("dve", lambda e: e.tensor_tensor(out=t1, in0=t1, in1=r1, op=ALU.mult), reads=[Tt1, Tr1], writes=[Tt1])
                P.op("dve", lambda e: e.scalar_tensor_tensor(out=o, in0=t1, scalar=neglam, in1=t0, op0=ALU.mult, op1=ALU.add),
                     reads=[Tt0, Tt1, T_LAM], writes=[To])
                pending.append([2, st3])

            def st3():
                P.op("act", lambda e: e.activation(out=sq, in_=o, func=AF.Square), reads=[To], writes=[Tsq])
                pending.append([1, st4])

            def st4():
                mm(psb(MSB), onesA, sq, True, True, reads=[T_CONST, Tsq], writes=[PB[MSB]])
                P.op("act", lambda e: e.activation(out=r0, in_=psb(MSB), func=AF.Ln, bias=EPS_AP), reads=[PB[MSB], T_CONST], writes=[Tr0])
                pending.append([1, st5])

            def st5():
                P.op("act", lambda e: e.activation(out=r1, in_=r0, func=AF.Exp, scale=-0.5), reads=[Tr0], writes=[Tr1])
                pending.append([1, st6])

            def st6():
                P.op("dve", lambda e: e.scalar_tensor_tensor(out=oT[:, c8, g * 512:(g + 1) * 512], in0=o, scalar=gsub, in1=r1,
                                                             op0=ALU.mult, op1=ALU.mult), reads=[To, Tr1, T_LAM], writes=[OT[c8][g]])

            pending.append([1, st1])

        def run_pending(force=False):
            if force:
                while pending:
                    pending.pop(0)[1]()
                return
            for it in list(pending):
                it[0] -= 1
                if it[0] <= 0:
                    pending.remove(it)
                    it[1]()

        th = []
        n = len(items)

        def pro():
            stageA(0)
            if n > 1:
                stageA(1)
        th.append(pro)
        for r in range(n):
            def f(r=r):
                if r + 2 < n:
                    stageA(r + 2)
                stageE(r)
                run_pending()
                if r >= 1:
                    stageB(r - 1)
            th.append(f)

        def epi():
            stageB(n - 1)
            if uidx == 7:
                run_pending(force=True)
        th.append(epi)
        return th

    def run_interleaved(main, bg):
        nb = len(bg)
        nm = max(1, len(main))
        done = 0
        for i, f in enumerate(main):
            f()
            want = ((i + 1) * nb + nm - 1) // nm
            while done < min(want, nb):
                bg[done]()
                done += 1
        while done < nb:
            bg[done]()
            done += 1

    NQ = [6, 6, 5, 5]
    aT = reg(O_O, 6 * TB).rearrange("p (c t) -> p c t", c=6)
    AT = [[Tile("aT%dt%d" % (j, tt), (O_O + j * TB + tt * 256, 256)) for tt in range(NT)] for j in range(6)]
    WDQ = [reg(O_W + i * 12288, 12288).rearrange("p (j n) -> p j n", j=6) for i in range(2)]
    T_WDQ = [Tile("wdq%d" % i, (O_W + i * 12288, 12288)) for i in range(2)]
    WGU = [reg(O_O + 6 * TB, 8192).rearrange("p (w c f) -> p w c f", w=2, c=KC),
           reg(O_M + 26624, 8192).rearrange("p (w c f) -> p w c f", w=2, c=KC)]
    T_WGU = [Tile("wgu0", (O_O + 6 * TB, 8192)), Tile("wgu1", (O_M + 26624, 8192))]
    SG = [reg(O_M + 34816, 2048, F32), reg(O_M + 36864, 2048, F32)]
    T_SG = [Tile("sg%d" % i, (O_M + 34816 + i * 2048, 2048)) for i in range(2)]
    assert 6 * TB + 8192 <= SZ_O and 38912 <= SZ_M

    def ffn_prefetch(layer):
        wg_v = wg_d[layer].rearrange("(c p) f -> p c f", p=128)
        wu_v = wu_d[layer].rearrange("(c p) f -> p c f", p=128)
        for bi in range(2):
            col0 = bi * 256
            P.op("pool", lambda e, bi=bi, col0=col0: e.dma_start(out=WGU[bi][:, 0, :, 0:256], in_=wg_v[:, :, col0:col0 + 256]),
                 writes=[T_WGU[bi]], dma=True)
            P.op("pool", lambda e, bi=bi, col0=col0: e.dma_start(out=WGU[bi][:, 1, :, 0:256], in_=wu_v[:, :, col0:col0 + 256]),
                 writes=[T_WGU[bi]], dma=True)

    def ffn(s, layer, ln_idx, last, prefetched=False, pre=None):
        wg_v = wg_d[layer].rearrange("(c p) f -> p c f", p=128)
        wu_v = wu_d[layer].rearrange("(c p) f -> p c f", p=128)
        wd_v = wd_d[layer].rearrange("(j p) n -> p j n", p=128)
        pre = list(pre or [])
        if pre:
            pre.pop(0)()
        else:
            load_ln_params(ln_idx)
        fblocks = []
        c_abs = 0
        qstart = []
        for q in range(4):
            qstart.append(c_abs)
            j = 0
            while j < NQ[q]:
                n = min(2, NQ[q] - j)
                fblocks.append((q, j, n, (c_abs + j) * 128))
                j += n
            c_abs += NQ[q]
        blk_i = [0]

        def load_block(bi):
            q, j0, n, col0 = fblocks[bi]
            buf = bi % 2
            P.op("pool", lambda e: e.dma_start(out=WGU[buf][:, 0, :, 0:n * 128], in_=wg_v[:, :, col0:col0 + n * 128]),
                 writes=[T_WGU[buf]], dma=True)
            P.op("pool", lambda e: e.dma_start(out=WGU[buf][:, 1, :, 0:n * 128], in_=wu_v[:, :, col0:col0 + n * 128]),
                 writes=[T_WGU[buf]], dma=True)

        def load_wd(q):
            buf = q % 2
            nq = NQ[q]
            P.op("pool", lambda e: e.dma_start(out=WDQ[buf][:, 0:nq, :], in_=wd_v[:, qstart[q]:qstart[q] + nq, :]),
                 writes=[T_WDQ[buf]], dma=True)

        if not prefetched:
            load_block(0)
            load_block(1)
        load_wd(0)
        gu_rr = [0]

        def gate_up(bi, jj, j, g):
            buf = bi % 2
            gb, ub = (0, 1) if gu_rr[0] % 2 == 0 else (2, 3)
            gu_rr[0] += 1
            for w_, bnk in ((0, gb), (1, ub)):
                for c in range(KC):
                    mm(psb(bnk), WGU[buf][:, w_, c, jj * 128:(jj + 1) * 128], hT[:, c, g * 512:(g + 1) * 512],
                       c == 0, c == KC - 1, reads=[T_WGU[buf]] + HT[4 * g:4 * g + 4], writes=[PB[bnk]])
            sgb, Tsg = SG[gu_rr[0] % 2], T_SG[gu_rr[0] % 2]
            P.op("act", lambda e: e.activation(out=sgb, in_=psb(gb), func=AF.Silu), reads=[PB[gb]], writes=[Tsg])
            P.op("dve", lambda e: e.tensor_tensor(out=aT[:, j, g * 512:(g + 1) * 512], in0=psb(ub), in1=sgb, op=ALU.mult),
                 reads=[PB[ub], Tsg], writes=AT[j][4 * g:4 * g + 4])

        for q in range(4):
            if q + 1 < 4:
                load_wd(q + 1)
            for bi, (bq, j0, n, col0) in enumerate(fblocks):
                if bq != q:
                    continue
                if pre:
                    for g in range(NG - 1):
                        for jj in range(n):
                            gate_up(bi, jj, j0 + jj, g)
                        if pre:
                            pre.pop(0)()
                    while pre:
                        pre.pop(0)()
                    load_ln_params(ln_idx)
                    for jj in range(n):
                        gate_up(bi, jj, j0 + jj, NG - 1)
                else:
                    for jj in range(n):
                        for g in range(NG):
                            gate_up(bi, jj, j0 + jj, g)
                if bi + 2 < len(fblocks):
                    load_block(bi + 2)
            nq = NQ[q]
            wbuf = q % 2

            def down_mm(tt, ba, bb, nq=nq, wbuf=wbuf, q=q):
                for half, bnk in ((0, ba), (1, bb)):
                    for j in range(nq):
                        mm(psb(bnk), aT[:, j, tt * 128:(tt + 1) * 128], WDQ[wbuf][:, j, half * 512:(half + 1) * 512],
                           j == 0, j == nq - 1, reads=[AT[j][tt], T_WDQ[wbuf]], writes=[PB[bnk]])

            if q < 3:
                for tt in range(NT):
                    ba, bb = (4, 5) if tt % 2 == 0 else (6, 7)
                    down_mm(tt, ba, bb)
                    hs = hres[:, tt, :]
                    if q == 0:
                        P.op("dve", lambda e, hs=hs, ba=ba: e.scalar_tensor_tensor(out=hs, in0=hs, scalar=ALPHA, in1=pswide(ba), op0=ALU.mult, op1=ALU.add),
                             reads=[HRES[tt], PB[ba], PB[bb]], writes=[HRES[tt]])
                    else:
                        P.op("dve", lambda e, hs=hs, ba=ba: e.tensor_tensor(out=hs, in0=hs, in1=pswide(ba), op=ALU.add),
                             reads=[HRES[tt], PB[ba], PB[bb]], writes=[HRES[tt]])
            else:
                def y_fn(tt, slot, ba, bb):
                    yv, Ty = LN_Y[slot], T_LNY[slot]
                    assert bb == ba + 1
                    P.op("dve", lambda e: e.tensor_tensor(out=yv, in0=hres[:, tt, :], in1=pswide(ba), op=ALU.add),
                         reads=[HRES[tt], PB[ba], PB[bb]], writes=[Ty])
                ln_site(s, down_mm, y_fn, ((2, 3), (4, 5), (6, 7)), (0, 1), want_T=not last, out_dram=last)

    def layer0(s):
        def xload(tt):
            xb, Txb = XB[tt % 4], T_XB[tt % 4]
            P.op("pool", lambda e, xb=xb, tt=tt: e.dma_start(out=xb, in_=x_d[s, tt * 128:(tt + 1) * 128, :]), writes=[Txb], dma=True)

        def vdf_proj(tt):
            b = 2 + tt % 2
            for c in range(KC):
                mm(psb(b), xT[:, c, tt * 128:(tt + 1) * 128], WV[:, c, :], c == 0, c == KC - 1, reads=[T_WV, XT[tt]], writes=[PB[b]])
            copy_op(evac_eng(), Vdf[:, tt, :], psb(b), [PB[b]], [VDF[tt]])

        for tt in range(min(4, NT)):
            xload(tt)
        P.op("pool", lambda e: e.dma_start(out=WV, in_=w_in_v[:, :, 2560:3072]), writes=[T_WV], dma=True)
        for S_ in sets:
            P.op("pool", lambda e, S_=S_: e.memset(S_.W[:, :, 320:448], 0.0), writes=[S_.TW])
        load_unit_weights(0, sets[0])
        for tt in range(NT):
            xb, Txb = XB[tt % 4], T_XB[tt % 4]
            if tt + 4 < NT:
                pass
            b = tt % 2
            for c in range(KC):
                P.op("pe", lambda e, c=c, xb=xb, b=b: e.transpose(psbf(b)[:, c * 128:(c + 1) * 128], xb[:, c * 128:(c + 1) * 128], ident),
                     reads=[Txb, T_CONST], writes=[PB[b]])
            eng = evac_eng()
            copy_op(eng, xT[:, :, tt * 128:(tt + 1) * 128], psbf(b)[:, 0:1024].rearrange("p (c t) -> p c t", c=KC), [PB[b]], [XT[tt]])
            if tt + 4 < NT:
                xload(tt + 4)
            if tt >= 2:
                vdf_proj(tt - 2)
        load_unit_weights(1, sets[1])
        for tt in range(max(0, NT - 2), NT):
            vdf_proj(tt)
        for f in proj_thunks(0, sets[0], (6, 7)):
            f()
        for u in range(8):
            S = sets[u % 2]
            if u + 1 < 8:
                bg = proj_thunks(u + 1, sets[(u + 1) % 2], (6, 7) if (u < 4) else (7,))
            else:
                bg = []
            main = sb_items(u, S) if u < 4 else df_items(u, S)
            if u == 7:
                P.op("pool", lambda e: e.dma_start(out=WOUT, in_=w_out_d.rearrange("(c p) n -> p c n", p=128)), writes=[T_WOUT], dma=True)
            run_interleaved(main, bg)
            if u + 2 < 8:
                load_unit_weights(u + 2, S)

    XRES = [reg(O_M + 26624, 4096, F32), reg(O_M + 30720, 4096, F32), reg(O_W + 16384, 4096, F32)]
    T_XRES = [Tile("xres0", (O_M + 26624, 4096)), Tile("xres1", (O_M + 30720, 4096)), Tile("xres2", (O_W + 16384, 4096))]
    WOUT = reg(O_W, 16384).rearrange("p (c n) -> p c n", c=KC)
    T_WOUT = Tile("wout", (O_W, 16384))

    def wout_ln1(s):
        load_ln_params(0)

        def mm_fn(tt, ba, bb):
            xr, Txr = XRES[tt % 3], T_XRES[tt % 3]
            P.op("sp", lambda e, xr=xr, tt=tt: e.dma_start(out=xr, in_=x_d[s, tt * 128:(tt + 1) * 128, :]), writes=[Txr], dma=True)
            for half, bnk in ((0, ba), (1, bb)):
                for c in range(KC):
                    mm(psb(bnk), oT[:, c, tt * 128:(tt + 1) * 128], WOUT[:, c, half * 512:(half + 1) * 512], c == 0, c == KC - 1,
                       reads=[OT[c][tt // 4], T_WOUT], writes=[PB[bnk]])

        def y_fn(tt, slot, ba, bb):
            xr, Txr = XRES[tt % 3], T_XRES[tt % 3]
            yv, Ty = LN_Y[slot], T_LNY[slot]
            assert bb == ba + 1
            P.op("dve", lambda e: e.scalar_tensor_tensor(out=yv, in0=xr, scalar=ALPHA, in1=pswide(ba), op0=ALU.mult, op1=ALU.add),
                 reads=[Txr, PB[ba], PB[bb]], writes=[Ty])

        return ln_site(s, mm_fn, y_fn, ((0, 1), (2, 3), (4, 5)), (6, 7), want_T=True, out_dram=False, defer=True)

    cvT = reg(O_O, 16384, F32).rearrange("p (c t) -> p c t", c=KC)
    CVT = [Tile("cv%d" % c, (O_O + c * 2048, 2048)) for c in range(KC)]
    UW = 544
    uT = reg(O_O + 16384, KC * UW * 2).rearrange("p (c t) -> p c t", c=KC)
    UT = [Tile("uT%d" % c, (O_O + 16384 + c * UW * 2, UW * 2)) for c in range(KC)]
    DG = [reg(O_W + i * 8192, 31 * 256).rearrange("p (k j) -> p k j", k=31) for i in range(2)]
    T_DG = [Tile("dg%d" % i, (O_W + i * 8192, 31 * 256)) for i in range(2)]
    _w1_off = (O_W + 16384, O_W + 20480, O_O + 16384 + KC * UW * 2)
    assert _w1_off[2] + 4096 <= O_O + SZ_O
    W1B = [reg(o, 4096).rearrange("p (c f) -> p c f", c=KC) for o in _w1_off]
    T_W1B = [Tile("w1b%d" % i, (_w1_off[i], 4096)) for i in range(3)]
    SIG = [reg(O_M + 20480, 2048, F32), reg(O_M + 22528, 2048, F32)]
    T_SIG = [Tile("sig%d" % i, (O_M + 20480 + i * 2048, 2048)) for i in range(2)]
    SQ = [reg(O_M + 24576, 2048, F32), reg(O_M + 26624, 2048, F32)]
    T_SQ = [Tile("sq%d" % i, (O_M + 24576 + i * 2048, 2048)) for i in range(2)]
    RSTD = reg(O_M + 28672, 2048, F32)
    T_RSTD = Tile("rstdc", (O_M + 28672, 2048))
    LNT = reg(O_M + 30720, 2048, F32)
    T_LNT = Tile("lnt", (O_M + 30720, 2048))
    DWB16 = reg(O_M + 32768, 512)
    ACCM = reg(O_M + 33792, 2048, F32)
    ACCV = reg(O_M + 35840, 2048, F32)
    T_ACCM = Tile("caccm", (O_M + 33792, 2048))
    T_ACCV = Tile("caccv", (O_M + 35840, 2048))
    T_DWB = Tile("dwb16", (O_M + 32768, 512))
    assert 16384 + KC * UW * 2 <= SZ_O
    pw1_v = pw1_d.rearrange("(c p) f -> p c f", p=128)

    def conv_mixer(s):
        P.op("dve", lambda e: e.tensor_copy(out=DWB16[:, 0:248], in_=VEC[:, V_DWW:V_DWW + 248]), reads=[T_CONST], writes=[T_DWB])
        seq = [(g, c) for g in range(NG) for c in range(KC)]
        N = len(seq)

        def load_w1(i):
            g, c = seq[i]
            buf = i % 3
            P.op("pool", lambda e: e.dma_start(out=W1B[buf][:, :, 0:128], in_=pw1_v[:, :, c * 128:(c + 1) * 128]), writes=[T_W1B[buf]], dma=True)
            P.op("pool", lambda e: e.dma_start(out=W1B[buf][:, :, 128:256], in_=pw1_v[:, :, D + c * 128:D + (c + 1) * 128]), writes=[T_W1B[buf]], dma=True)

        def build_diag(i):
            g, c = seq[i]
            dg, Tdg = DG[i % 2], T_DG[i % 2]
            P.op("dve", lambda e: e.tensor_tensor(
                out=dg, in0=ident.rearrange("p (o j) -> p o j", o=1).broadcast_to([128, 31, 128]),
                in1=DWB16[:, c * 31:(c + 1) * 31].rearrange("p (k o) -> p k o", o=1).broadcast_to([128, 31, 128]), op=ALU.mult),
                reads=[T_CONST, T_DWB], writes=[Tdg])

        def banks(i):
            return (0, 1) if i % 2 == 0 else (2, 3)

        def Pst(i):
            g, c = seq[i]
            buf = i % 3
            vb, gbk = banks(i)
            for co, bnk in ((0, vb), (128, gbk)):
                for k in range(KC):
                    mm(psb(bnk), W1B[buf][:, k, co:co + 128], hT[:, k, g * 512:(g + 1) * 512], k == 0, k == KC - 1,
                       reads=[T_W1B[buf]] + HT[4 * g:4 * g + 4], writes=[PB[bnk]])
            if i + 3 < N:
                load_w1(i + 3)

        def Est(i):
            g, c = seq[i]
            vb, gbk = banks(i)
            sg, Tsg = SIG[i % 2], T_SIG[i % 2]
            P.op("act", lambda e: e.activation(out=sg, in_=psb(gbk), func=AF.Sigmoid, bias=VEC[:, V_PW1BG + c:V_PW1BG + c + 1]),
                 reads=[PB[gbk], T_CONST], writes=[Tsg])
            if g == 0:
                P.op("dve", lambda e: e.memset(uT[:, c, 0:32], 0.0), writes=[UT[c]])
            else:
                P.op("dve", lambda e: e.tensor_copy(out=uT[:, c, 0:32], in_=uT[:, c, 512:544]), reads=[UT[c]], writes=[UT[c]])
            P.op("dve", lambda e: e.scalar_tensor_tensor(out=uT[:, c, 32:544], in0=psb(vb), scalar=VEC[:, V_PW1BV + c:V_PW1BV + c + 1],
                                                         in1=sg, op0=ALU.add, op1=ALU.mult),
                 reads=[PB[vb], Tsg, T_CONST, UT[c]], writes=[UT[c]])
            if i + 1 < N:
                build_diag(i + 1)

        MB, VB = 6, 7

        def Cmm(i):
            g, c = seq[i]
            dg, Tdg = DG[i % 2], T_DG[i % 2]
            cb_ = 4 + i % 2
            for k in range(31):
                mm(psb(cb_), dg[:, k, :], uT[:, c, 2 + k:2 + k + 512], k == 0, k == 30, reads=[Tdg, UT[c]], writes=[PB[cb_]])

        def Cev(i):
            g, c = seq[i]
            cb_ = 4 + i % 2
            P.op("act", lambda e: e.activation(out=cvT[:, c, :], in_=psb(cb_), func=AF.Identity, bias=VEC[:, V_DWB + c:V_DWB + c + 1]),
                 reads=[PB[cb_], T_CONST], writes=[CVT[c]])
            sq, Tsq = SQ[c % 2], T_SQ[c % 2]
            P.op("act", lambda e: e.activation(out=sq, in_=cvT[:, c, :], func=AF.Square), reads=[CVT[c]], writes=[Tsq])
            if c == 1:
                P.op("dve", lambda e: e.tensor_tensor(out=ACCM, in0=cvT[:, 0, :], in1=cvT[:, 1, :], op=ALU.add),
                     reads=[CVT[0], CVT[1]], writes=[T_ACCM])
                P.op("dve", lambda e: e.tensor_tensor(out=ACCV, in0=SQ[0], in1=SQ[1], op=ALU.add),
                     reads=[T_SQ[0], T_SQ[1]], writes=[T_ACCV])
            elif c >= 2:
                P.op("dve", lambda e: e.tensor_tensor(out=ACCM, in0=ACCM, in1=cvT[:, c, :], op=ALU.add),
                     reads=[T_ACCM, CVT[c]], writes=[T_ACCM])
                P.op("dve", lambda e: e.tensor_tensor(out=ACCV, in0=ACCV, in1=sq, op=ALU.add),
                     reads=[T_ACCV, Tsq], writes=[T_ACCV])

        def stats_mm(i):
            mm(psb(MB), onesB, ACCM, True, True, reads=[T_CONST, T_ACCM], writes=[PB[MB]])
            mm(psb(VB), onesB, ACCV, True, True, reads=[T_CONST, T_ACCV], writes=[PB[VB]])

        def LNfin(g):
            P.op("act", lambda e: e.activation(out=LNT, in_=psb(MB), func=AF.Copy), reads=[PB[MB]], writes=[T_LNT])
            P.op("dve", lambda e: e.tensor_tensor(out=RSTD, in0=LNT, in1=LNT, op=ALU.mult), reads=[T_LNT], writes=[T_RSTD])
            P.op("dve", lambda e: e.tensor_tensor(out=RSTD, in0=psb(VB), in1=RSTD, op=ALU.subtract), reads=[PB[VB], T_RSTD], writes=[T_RSTD])
            P.op("act", lambda e: e.activation(out=RSTD, in_=RSTD, func=AF.Ln, bias=EPS_AP), reads=[T_RSTD, T_CONST], writes=[T_RSTD])
            P.op("act", lambda e: e.activation(out=RSTD, in_=RSTD, func=AF.Exp, scale=-0.5), reads=[T_RSTD], writes=[T_RSTD])

        def LNchunk(g, c):
            P.op("dve", lambda e: e.tensor_tensor(out=cvT[:, c, :], in0=cvT[:, c, :], in1=LNT, op=ALU.subtract),
                 reads=[CVT[c], T_LNT], writes=[CVT[c]])
            P.op("dve", lambda e: e.scalar_tensor_tensor(out=cvT[:, c, :], in0=cvT[:, c, :], scalar=VEC[:, V_CLNG + c:V_CLNG + c + 1],
                                                         in1=RSTD, op0=ALU.mult, op1=ALU.mult),
                 reads=[CVT[c], T_RSTD, T_CONST], writes=[CVT[c]])
            P.op("act", lambda e: e.activation(out=hT[:, c, g * 512:(g + 1) * 512], in_=cvT[:, c, :], func=AF.Silu,
                                               bias=VEC[:, V_CLNB + c:V_CLNB + c + 1]),
                 reads=[CVT[c], T_CONST], writes=HT[4 * g:4 * g + 4])

        load_w1(0)
        load_w1(1)
        if N > 2:
            load_w1(2)
        build_diag(0)
        Pst(0)
        done_P = {0}
        done_E = set()
        lnq = []
        for i in range(N):
            g, c = seq[i]
            if i + 1 < N and (i + 1) not in done_P:
                Pst(i + 1)
                done_P.add(i + 1)
            if i not in done_E:
                Est(i)
                done_E.add(i)
            for _ in range(2):
                if lnq:
                    lnq.pop(0)()
            Cmm(i)
            Cev(i)
            if c == KC - 1:
                if i + 2 < N:
                    Pst(i + 2)
                    done_P.add(i + 2)
                stats_mm(i)
                if i + 1 < N:
                    Est(i + 1)
                    done_E.add(i + 1)
                LNfin(g)
                for cc in range(KC):
                    lnq.append(lambda g=g, cc=cc: LNchunk(g, cc))
                for _ in range(2 if i + 1 < N else KC):
                    lnq.pop(0)()
        assert not lnq

    WPW2 = reg(O_W, 16384).rearrange("p (c n) -> p c n", c=KC)
    T_WPW2 = Tile("wpw2", (O_W, 16384))
    B2T = reg(O_O, 4096, F32)
    T_B2 = Tile("b2t", (O_O, 4096))

    def pw2_ln(s):
        P.op("pool", lambda e: e.dma_start(out=WPW2, in_=pw2_d.rearrange("(c p) n -> p c n", p=128)), writes=[T_WPW2], dma=True)
        P.op("sp", lambda e: e.dma_start(out=B2T, in_=pw2b_d[0:1, :].partition_broadcast(128)), writes=[T_B2], dma=True)
        load_ln_params(2)
        ffn_prefetch(1)

        def mm_fn(tt, ba, bb):
            for half, bnk in ((0, ba), (1, bb)):
                for c in range(KC):
                    mm(psb(bnk), hT[:, c, tt * 128:(tt + 1) * 128], WPW2[:, c, half * 512:(half + 1) * 512], c == 0, False,
                       reads=[HT[tt], T_WPW2], writes=[PB[bnk]])
                mm(psb(bnk), onesA, B2T[:, half * 512:(half + 1) * 512], False, True, reads=[T_CONST, T_B2], writes=[PB[bnk]])

        def y_fn(tt, slot, ba, bb):
            yv, Ty = LN_Y[slot], T_LNY[slot]
            assert bb == ba + 1
            P.op("dve", lambda e: e.scalar_tensor_tensor(out=yv, in0=hres[:, tt, :], scalar=ALPHA, in1=pswide(ba), op0=ALU.mult, op1=ALU.add),
                 reads=[HRES[tt], PB[ba], PB[bb]], writes=[Ty])

        return ln_site(s, mm_fn, y_fn, ((0, 1), (2, 3), (4, 5)), (6, 7), want_T=True, out_dram=False, defer=True)

    def dump_f32(view_tiles):
        P.barrier()
        P.op("sp", lambda e: e.dma_start(out=dbg_d, in_=reg(O_H, NT * 4096, F32)), dma=True)

    def dump_bf(off):
        P.barrier()
        P.op("sp", lambda e: e.dma_start(out=dbgb_d, in_=reg(off, KC * TB)), dma=True)

    stages = ["attn", "ln1", "ffn0", "conv", "ln3", "all"]
    lim = stages.index(upto)
    USE_BARRIERS = True

    def pb():
        if USE_BARRIERS:
            P.barrier()

    for s in range(NSEQ):
        layer0(s)
        if lim == 0:
            dump_bf(O_O)
            break
        pre = wout_ln1(s)
        if lim == 1:
            for f in pre:
                f()
            dump_f32(None)
            dump_bf(O_HT)
            break
        ffn(s, 0, 1, last=False, pre=pre)
        if lim == 2:
            dump_f32(None)
            dump_bf(O_HT)
            break
        conv_mixer(s)
        if lim == 3:
            dump_bf(O_HT)
            break
        pre = pw2_ln(s)
        if lim == 4:
            for f in pre:
                f()
            dump_f32(None)
            dump_bf(O_HT)
            break
        ffn(s, 1, 3, last=True, prefetched=True, pre=pre)
    P.barrier()
    P.emit(nc)
    return nc, P


def host_consts(T):
    bf = ml_dtypes.bfloat16
    j = np.arange(128)[:, None]
    k = np.arange(128)[None, :]
    cb = np.zeros((128, 2048), np.float32)
    cb[:, 0:128] = (j == k)
    cb[:, 128:256] = 1.0
    cb[:, 256:384] = -(j >= k).astype(np.float32)
    cb[:, 384:512] = -1.0
    cb[:, 512:640] = np.where(j >= k, NEG, 0.0)
    cb[:, 1024:1152] = np.where(j > k, NEG, 0.0)
    cf = np.zeros((128, 512), np.float32)
    cf[:, 0:128] = 1.0 / 128
    cf[:, 128:256] = 1.0 / 1024
    cf[:, 256:384] = np.eye(128, dtype=np.float32) * np.float32(ALPHA)
    cf[:, 384:512] = np.eye(128, dtype=np.float32)
    t = np.arange(T)
    aug = np.zeros((4, 2, 4, T), np.float32)
    for h in range(4):
        sl = SLOPES[h]
        aug[h, 0, 0] = 1.0
        aug[h, 0, 1] = 1.0
        aug[h, 0, 2] = -sl * 128 * (t // 128)
        aug[h, 0, 3] = -sl * (t % 128)
        aug[h, 1, 0] = sl * 128 * (t // 128)
        aug[h, 1, 1] = sl * (t % 128)
        aug[h, 1, 2] = 1.0
        aug[h, 1, 3] = 1.0
    return cb.astype(bf), cf, aug.astype(bf)


def pack_inputs(inp, T):
    f = lambda a: np.ascontiguousarray(np.asarray(a, dtype=np.float32))
    cb, cf, aug = host_consts(T)
    vec = np.zeros((128, 560), np.float32)
    pw1b = f(inp["conv_pw1_b"])[0]
    vec[:, 0:8] = pw1b[:D].reshape(8, 128).T
    vec[:, 8:16] = pw1b[D:].reshape(8, 128).T
    vec[:, 16:24] = f(inp["conv_dw_b"])[0].reshape(8, 128).T
    vec[:, 24:32] = f(inp["conv_ln_g"])[0].reshape(8, 128).T
    vec[:, 32:40] = f(inp["conv_ln_b"])[0].reshape(8, 128).T
    dww = f(inp["conv_dw_w"])[0, :, 0, :]
    vec[:, 40:288] = dww.reshape(31, 8, 128).transpose(2, 1, 0).reshape(128, 248)
    vec[:, 288] = f(inp["diff_subln_g"])[0]
    vec[:, 289:353] = f(inp["diff_lambda_q1"])[0][None, :]
    vec[:, 353:417] = f(inp["diff_lambda_k1"])[0][None, :]
    vec[:, 417:481] = f(inp["diff_lambda_q2"])[0][None, :]
    vec[:, 481:545] = f(inp["diff_lambda_k2"])[0][None, :]
    ln_g = np.stack([f(inp["mix_ln_g"])[0], f(inp["ffn_ln_g"])[0], f(inp["mix_ln_g"])[1], f(inp["ffn_ln_g"])[1]])
    ln_b = np.stack([f(inp["mix_ln_b"])[0], f(inp["ffn_ln_b"])[0], f(inp["mix_ln_b"])[1], f(inp["ffn_ln_b"])[1]])
    shared = {
        "attn_w_in": f(inp["attn_w_in"])[0], "attn_w_out": f(inp["attn_w_out"])[0],
        "conv_pw1_w": f(inp["conv_pw1_w"])[0], "conv_pw2_w": f(inp["conv_pw2_w"])[0],
        "ffn_w_gate": f(inp["ffn_w_gate"]), "ffn_w_up": f(inp["ffn_w_up"]), "ffn_w_down": f(inp["ffn_w_down"]),
        "ln_g": np.ascontiguousarray(ln_g), "ln_b": np.ascontiguousarray(ln_b),
        "pw2_b": f(inp["conv_pw2_b"]).reshape(1, D),
        "cb": cb, "cf": cf, "vec": vec, "aug": aug,
    }
    return shared


_CACHE = {}


def kernel(**inputs):
    x = np.ascontiguousarray(np.asarray(inputs["x"], dtype=np.float32))
    B, T, _ = x.shape
    ncores = 8
    nseq = B // ncores
    key = (T, nseq)
    if key not in _CACHE:
        _CACHE[key] = build_program(T, nseq, "all")[0]
    nc = _CACHE[key]
    shared = pack_inputs(inputs, T)
    in_maps = []
    for c in range(ncores):
        m = dict(shared)
        m["x"] = np.ascontiguousarray(x[c * nseq:(c + 1) * nseq])
        in_maps.append(m)
    res = run_bass_kernel_spmd(nc, in_maps, core_ids=list(range(ncores)))
    out = np.concatenate([np.asarray(r["out"], dtype=np.float32) for r in res.results], axis=0)
    return out
```
